# Optimizing a Trainium2 kernel written in Bass

```python
import math
import jax, jax.numpy as jnp
from jax import lax
import numpy as np

D_MODEL = 2048
BATCH = 4
SEQ = 8192
DEPTH = 4
DEC_BATCH = 32
DEC_SEQ = 64
PAST_LEN = 4096

CHUNK = 64
N_META = 16
N_MIXERS = 2
N_CONV_LAYERS = (DEPTH + 1) // 2
N_SSD_LAYERS = DEPTH // 2
SC_WIDTH = 3
SSD_INNER = 2 * D_MODEL
SSD_HEADDIM = 64
SSD_HEADS = SSD_INNER // SSD_HEADDIM
SSD_GROUPS = 8
SSD_STATE = 128
SSD_CONV_WIDTH = 4
SSD_CONV_DIM = SSD_INNER + 2 * SSD_GROUPS * SSD_STATE
SSD_BLOCK = 64
D_FF = 5632
FFN_CONV_WIDTH = 3
EPS = 1e-6

kernel_name = 'hybrid_shortconv_ssd_convffn_stream_step'


def rms_norm(x, w):
    xf = x.astype(jnp.float32)
    y = xf * lax.rsqrt(jnp.mean(xf * xf, axis=-1, keepdims=True) + EPS)
    return (y * w.astype(jnp.float32)).astype(x.dtype)


def causal_dwconv(u, buf, w):
    width = w.shape[0]
    L = u.shape[1]
    full = jnp.concatenate([buf.astype(u.dtype), u], axis=1)
    y = sum(w[k] * full[:, k:k + L] for k in range(width))
    return y, full[:, L:]


def pad_seq(t, pad):
    return jnp.pad(t, [(0, 0), (0, pad)] + [(0, 0)] * (t.ndim - 2))


def short_conv_mixer(h, buf, w_in, conv_w, w_out):
    b_gate, c_gate, v = jnp.split(h @ w_in, 3, axis=-1)
    y, new_buf = causal_dwconv(c_gate * v, buf, conv_w)
    return (b_gate * y) @ w_out, new_buf


def ssd_scan(x, dt, A, Bm, Cm, state0):
    b, L = x.shape[:2]
    nc = L // SSD_BLOCK
    hpg = SSD_HEADS // SSD_GROUPS

    def blocks(t):
        return jnp.moveaxis(t.reshape((b, nc, SSD_BLOCK) + t.shape[2:]), 1, 0)

    xs = blocks(x.reshape(b, L, SSD_GROUPS, hpg, SSD_HEADDIM))
    dts = blocks(dt.reshape(b, L, SSD_GROUPS, hpg))
    Bs = blocks(Bm)
    Cs = blocks(Cm)
    Ag = A.reshape(SSD_GROUPS, hpg)
    mask = jnp.tril(jnp.ones((SSD_BLOCK, SSD_BLOCK), dtype=bool))[None, :, :, None, None]

    def step(state, blk):
        xb, dtb, Bb, Cb = blk
        cs = jnp.cumsum(dtb * Ag, axis=1)
        seg = cs[:, :, None] - cs[:, None, :]
        decay = jnp.exp(jnp.where(mask, seg, -jnp.inf))
        w_ij = decay * dtb[:, None]
        cb = jnp.einsum('bign,bjgn->bijg', Cb, Bb)
        y_intra = jnp.einsum('bijg,bijgh,bjghp->bighp', cb, w_ij, xb)
        y_inter = jnp.einsum('bign,bghpn->bighp', Cb, state) * jnp.exp(cs)[..., None]
        last = cs[:, -1]
        w_tail = jnp.exp(last[:, None] - cs) * dtb
        new_state = state * jnp.exp(last)[..., None, None] + jnp.einsum(
            'bjgn,bjgh,bjghp->bghpn', Bb, w_tail, xb)
        return new_state, y_intra + y_inter

    state_g = state0.astype(jnp.float32).reshape(b, SSD_GROUPS, hpg, SSD_HEADDIM, SSD_STATE)
    final, ys = lax.scan(step, state_g, (xs, dts, Bs, Cs))
    y = jnp.moveaxis(ys, 0, 1).reshape(b, L, SSD_HEADS, SSD_HEADDIM)
    return y, final.reshape(b, SSD_HEADS, SSD_HEADDIM, SSD_STATE)


def ssd_mixer(h, conv_buf, ssm_state, w_in, conv_w, conv_b, dt_bias, a_log, d_skip, norm_w, w_out):
    b, L = h.shape[:2]
    f32 = jnp.float32
    z, xbc, dt = jnp.split(h @ w_in, [SSD_INNER, SSD_INNER + SSD_CONV_DIM], axis=-1)
    xbc, new_conv = causal_dwconv(xbc, conv_buf, conv_w)
    xbc = jax.nn.silu(xbc + conv_b)
    xs, Bm, Cm = jnp.split(xbc, [SSD_INNER, SSD_INNER + SSD_GROUPS * SSD_STATE], axis=-1)
    xs = xs.astype(f32).reshape(b, L, SSD_HEADS, SSD_HEADDIM)
    Bm = Bm.astype(f32).reshape(b, L, SSD_GROUPS, SSD_STATE)
    Cm = Cm.astype(f32).reshape(b, L, SSD_GROUPS, SSD_STATE)
    dt = jax.nn.softplus(dt.astype(f32) + dt_bias.astype(f32))
    A = -jnp.exp(a_log.astype(f32))
    pad = (-L) % SSD_BLOCK
    y, new_state = ssd_scan(pad_seq(xs, pad), pad_seq(dt, pad), A,
                            pad_seq(Bm, pad), pad_seq(Cm, pad), ssm_state)
    y = y[:, :L] + d_skip.astype(f32)[:, None] * xs
    g = y.reshape(b, L, SSD_INNER) * jax.nn.silu(z.astype(f32))
    g = g.reshape(b, L, SSD_GROUPS, SSD_INNER // SSD_GROUPS)
    g = g * lax.rsqrt(jnp.mean(g * g, axis=-1, keepdims=True) + EPS)
    g = (g.reshape(b, L, SSD_INNER) * norm_w.astype(f32)).astype(h.dtype)
    return g @ w_out, new_conv, new_state


def conv_ffn(h, buf, w_up, conv_w, conv_b, w_down):
    u, new_buf = causal_dwconv(h @ w_up, buf, conv_w)
    a, v = jnp.split(u + conv_b, 2, axis=-1)
    return (jax.nn.silu(a) * v) @ w_down, new_buf


def trunk(x, conv_a_buf, ssd_conv_buf, ssd_state, ffn_buf,
          norm_mix, norm_ffn, norm_final, sc_w_in, sc_conv_w, sc_w_out,
          ssd_w_in, ssd_conv_w, ssd_conv_b, ssd_dt_bias, ssd_a_log, ssd_d, ssd_norm_w, ssd_w_out,
          ffn_w_up, ffn_conv_w, ffn_conv_b, ffn_w_down):
    new_conv_a, new_ssd_conv, new_ssd, new_ffn = [], [], [], []
    for i in range(DEPTH):
        h = rms_norm(x, norm_mix[i])
        j = i // N_MIXERS
        if i % N_MIXERS == 0:
            out, nb = short_conv_mixer(h, conv_a_buf[j], sc_w_in[j], sc_conv_w[j], sc_w_out[j])
            new_conv_a.append(nb)
        else:
            out, nc, ns = ssd_mixer(h, ssd_conv_buf[j], ssd_state[j], ssd_w_in[j], ssd_conv_w[j],
                                    ssd_conv_b[j], ssd_dt_bias[j], ssd_a_log[j], ssd_d[j],
                                    ssd_norm_w[j], ssd_w_out[j])
            new_ssd_conv.append(nc)
            new_ssd.append(ns)
        x = x + out
        h = rms_norm(x, norm_ffn[i])
        out, nf = conv_ffn(h, ffn_buf[i], ffn_w_up[i], ffn_conv_w[i], ffn_conv_b[i], ffn_w_down[i])
        new_ffn.append(nf)
        x = x + out
    x = rms_norm(x, norm_final)
    return (x, jnp.stack(new_conv_a), jnp.stack(new_ssd_conv), jnp.stack(new_ssd), jnp.stack(new_ffn))


def setup_inputs(seed: int = 0) -> dict:
    key = jax.random.key(seed)
    ks = jax.random.split(key, 32)
    f32 = jnp.float32

    def nrm(k, shape, scale):
        return jax.random.normal(k, shape, f32) * scale

    res = (2 * DEPTH) ** -0.5
    dt0 = jnp.exp(jax.random.uniform(ks[20], (N_SSD_LAYERS, SSD_HEADS), f32,
                                     math.log(1e-3), math.log(1e-1)))
    return {
        'x_prompt': nrm(ks[0], (BATCH, SEQ, D_MODEL), 1.0),
        'x_sample': nrm(ks[1], (DEC_BATCH, DEC_SEQ, D_MODEL), 1.0),
        'state_conv_a': nrm(ks[2], (N_CONV_LAYERS, DEC_BATCH, SC_WIDTH - 1, D_MODEL), 1.0),
        'state_ssd_conv': nrm(ks[3], (N_SSD_LAYERS, DEC_BATCH, SSD_CONV_WIDTH - 1, SSD_CONV_DIM), 1.0),
        'state_ssd': nrm(ks[4], (N_SSD_LAYERS, DEC_BATCH, SSD_HEADS, SSD_HEADDIM, SSD_STATE), 0.1),
        'state_ffn_conv': nrm(ks[5], (DEPTH, DEC_BATCH, FFN_CONV_WIDTH - 1, 2 * D_FF), 1.0),
        'meta_tokens': nrm(ks[6], (N_META, D_MODEL), 1.0),
        'norm_mix': 1.0 + nrm(ks[7], (DEPTH, D_MODEL), 0.02),
        'norm_ffn': 1.0 + nrm(ks[8], (DEPTH, D_MODEL), 0.02),
        'norm_final': 1.0 + nrm(ks[9], (D_MODEL,), 0.02),
        'sc_w_in': nrm(ks[10], (N_CONV_LAYERS, D_MODEL, 3 * D_MODEL), D_MODEL ** -0.5),
        'sc_conv_w': nrm(ks[11], (N_CONV_LAYERS, SC_WIDTH, D_MODEL), SC_WIDTH ** -0.5),
        'sc_w_out': nrm(ks[12], (N_CONV_LAYERS, D_MODEL, D_MODEL), res * D_MODEL ** -0.5),
        'ssd_w_in': nrm(ks[13], (N_SSD_LAYERS, D_MODEL, 2 * SSD_INNER + 2 * SSD_GROUPS * SSD_STATE + SSD_HEADS),
                        D_MODEL ** -0.5),
        'ssd_conv_w': nrm(ks[14], (N_SSD_LAYERS, SSD_CONV_WIDTH, SSD_CONV_DIM), SSD_CONV_WIDTH ** -0.5),
        'ssd_conv_b': nrm(ks[15], (N_SSD_LAYERS, SSD_CONV_DIM), 0.02),
        'ssd_dt_bias': dt0 + jnp.log(-jnp.expm1(-dt0)),
        'ssd_a_log': jnp.log(jax.random.uniform(ks[16], (N_SSD_LAYERS, SSD_HEADS), f32, 1.0, 16.0)),
        'ssd_d': 1.0 + nrm(ks[17], (N_SSD_LAYERS, SSD_HEADS), 0.02),
        'ssd_norm_w': 1.0 + nrm(ks[18], (N_SSD_LAYERS, SSD_INNER), 0.02),
        'ssd_w_out': nrm(ks[19], (N_SSD_LAYERS, SSD_INNER, D_MODEL), res * SSD_INNER ** -0.5),
        'ffn_w_up': nrm(ks[21], (DEPTH, D_MODEL, 2 * D_FF), D_MODEL ** -0.5),
        'ffn_conv_w': nrm(ks[22], (DEPTH, FFN_CONV_WIDTH, 2 * D_FF), FFN_CONV_WIDTH ** -0.5),
        'ffn_conv_b': nrm(ks[23], (DEPTH, 2 * D_FF), 0.02),
        'ffn_w_down': nrm(ks[24], (DEPTH, D_FF, D_MODEL), res * D_FF ** -0.5),
    }


def reference(x_prompt, x_sample, state_conv_a, state_ssd_conv, state_ssd, state_ffn_conv,
              meta_tokens, norm_mix, norm_ffn, norm_final, sc_w_in, sc_conv_w, sc_w_out,
              ssd_w_in, ssd_conv_w, ssd_conv_b, ssd_dt_bias, ssd_a_log, ssd_d, ssd_norm_w, ssd_w_out,
              ffn_w_up, ffn_conv_w, ffn_conv_b, ffn_w_down):
    b = x_prompt.shape[0]
    dtype = x_prompt.dtype
    meta = jnp.broadcast_to(meta_tokens.astype(dtype)[None], (b, N_META, D_MODEL))
    xp = jnp.concatenate([meta, x_prompt], axis=1)
    zero_conv_a = jnp.zeros((N_CONV_LAYERS, b, SC_WIDTH - 1, D_MODEL), dtype)
    zero_ssd_conv = jnp.zeros((N_SSD_LAYERS, b, SSD_CONV_WIDTH - 1, SSD_CONV_DIM), dtype)
    zero_ssd = jnp.zeros((N_SSD_LAYERS, b, SSD_HEADS, SSD_HEADDIM, SSD_STATE), jnp.float32)
    zero_ffn = jnp.zeros((DEPTH, b, FFN_CONV_WIDTH - 1, 2 * D_FF), dtype)
    yp, p_conv_a, p_ssd_conv, p_ssd, p_ffn_conv = trunk(
        xp, zero_conv_a, zero_ssd_conv, zero_ssd, zero_ffn,
        norm_mix, norm_ffn, norm_final, sc_w_in, sc_conv_w, sc_w_out,
        ssd_w_in, ssd_conv_w, ssd_conv_b, ssd_dt_bias, ssd_a_log, ssd_d, ssd_norm_w, ssd_w_out,
        ffn_w_up, ffn_conv_w, ffn_conv_b, ffn_w_down)
    y_prompt = yp[:, N_META:]
    y_sample, s_conv_a, s_ssd_conv, s_ssd, s_ffn_conv = trunk(
        x_sample, state_conv_a, state_ssd_conv, state_ssd, state_ffn_conv,
        norm_mix, norm_ffn, norm_final, sc_w_in, sc_conv_w, sc_w_out,
        ssd_w_in, ssd_conv_w, ssd_conv_b, ssd_dt_bias, ssd_a_log, ssd_d, ssd_norm_w, ssd_w_out,
        ffn_w_up, ffn_conv_w, ffn_conv_b, ffn_w_down)
    return (y_prompt, y_sample, p_conv_a, p_ssd_conv, p_ssd, p_ffn_conv,
            s_conv_a, s_ssd_conv, s_ssd, s_ffn_conv)
```

```python
import contextlib
import numpy as np
import concourse.bass as bass
import concourse.mybir as mybir
from concourse.bass_utils import run_bass_kernel_spmd

F32 = mybir.dt.float32
BF16 = mybir.dt.bfloat16
ALU = mybir.AluOpType
AF = mybir.ActivationFunctionType

D = 2048
DC = 16
DFF = 5632
FC = 44
HEADS = 64
GROUPS = 8
NSTATE = 128
INNER = 4096
CONVD = 6144
EPS = 1e-6
NEG = -1.0e30


class Op:
    __slots__ = ("eng", "fn", "deps", "signal", "dma", "slot", "seq", "cnt")

    def __init__(self, eng, fn, dma):
        self.eng = eng
        self.fn = fn
        self.dma = dma
        self.deps = []
        self.signal = False
        self.slot = -1
        self.seq = 0
        self.cnt = 0


class Sched:
    ENGS = ("pe", "act", "dve", "pool", "sp")

    def __init__(self, nslots=24):
        self.ops = {e: [] for e in self.ENGS}
        self.lastw = {}
        self.readers = {}
        self.fence = []
        self.nslots = nslots
        self.slot_ops = [[] for _ in range(nslots)]
        self.rr = 0
        self.rr_pool = 0
        self.nops = 0

    def add(self, eng, fn, reads=(), writes=(), dma=False, nobar=False):
        op = Op(eng, fn, dma)
        deps = {}

        def need(o):
            if o is None or o is op:
                return
            if (not o.dma) and (not dma) and o.eng == "pe" and eng == "pe":
                return
            deps[id(o)] = o

        for r in reads:
            need(self.lastw.get(r))
            if isinstance(r, tuple) and r[0] == "ps":
                rd = self.readers.get(r)
                if rd:
                    for o in rd.values():
                        if o.eng != eng:
                            need(o)
        for w in writes:
            need(self.lastw.get(w))
            rd = self.readers.get(w)
            if rd:
                for o in rd.values():
                    need(o)
        if not nobar:
            for o in self.fence:
                need(o)
        if dma:
            half = self.nslots // 2
            if eng == "pool":
                s = half + (self.rr_pool % half)
                self.rr_pool += 1
            else:
                s = self.rr % half
                self.rr += 1
            if self.slot_ops[s]:
                need(self.slot_ops[s][-1])
            self.slot_ops[s].append(op)
            op.slot = s
            op.seq = len(self.slot_ops[s])
        op.deps = list(deps.values())
        for o in op.deps:
            o.signal = True
        for r in reads:
            d = self.readers.setdefault(r, {})
            d[id(op) if dma else eng] = op
        for w in writes:
            self.lastw[w] = op
            self.readers[w] = {}
        self.ops[eng].append(op)
        self.nops += 1
        return op

    def barrier(self):
        self.fence = [self.ops[e][-1] for e in ("pe", "act", "dve", "sp") if self.ops[e]]

    def emit(self, nc, stack):
        sems = {e: stack.enter_context(nc.semaphore("sem_" + e)) for e in self.ENGS}
        slot_sems = [stack.enter_context(nc.semaphore("dsl%d" % i)) for i in range(self.nslots)]
        for e in self.ENGS:
            if self.ops[e]:
                last = self.ops[e][-1]
                if not last.dma:
                    last.signal = True
            c = 0
            for op in self.ops[e]:
                if (not op.dma) and op.signal:
                    c += 1
                    op.cnt = c
        final = {}
        for e in self.ENGS:
            n = 0
            for op in self.ops[e]:
                if not op.dma and op.signal:
                    n = op.cnt
            final[e] = n

        def ticket(o):
            if o.dma:
                return slot_sems[o.slot], 16 * o.seq
            return sems[o.eng], o.cnt

        def run(ename, eng):
            known = {}
            for op in self.ops[ename]:
                for o in op.deps:
                    s, v = ticket(o)
                    k = id(s)
                    if known.get(k, 0) < v:
                        eng.wait_ge(s, v)
                        known[k] = v
                ins = op.fn(eng)
                if op.dma:
                    ins.then_inc(slot_sems[op.slot], 16)
                elif op.signal:
                    ins.then_inc(sems[ename], 1)
            if ename == "sp":
                for i, so in enumerate(self.slot_ops):
                    if so:
                        eng.wait_ge(slot_sems[i], 16 * len(so))
                for e2 in ("pe", "act", "dve", "pool"):
                    if final[e2] > 0:
                        eng.wait_ge(sems[e2], final[e2])

        with nc.Block() as block:
            @block.tensor
            def _(e):
                run("pe", e)

            @block.scalar
            def _(e):
                run("act", e)

            @block.vector
            def _(e):
                run("dve", e)

            @block.gpsimd
            def _(e):
                run("pool", e)

            @block.sync
            def _(e):
                run("sp", e)


class StopBuild(Exception):
    pass


import os as _os
_STOP = _os.environ.get("SSD_STOP", "")


class Cfg:
    def __init__(self, npt=8208, tt=456, qp=114, nseg=4, sl=64, nl=4):
        self.NPT = npt
        self.TT = tt
        self.QP = qp
        self.NSEG = nseg
        self.SL = sl
        self.NL = nl
        assert npt % tt == 0 and tt % qp == 0 and tt <= 512 and qp <= 128
        assert nseg * sl <= 512
        self.NTOK = npt + nseg * sl
        self.NCONV = (nl + 1) // 2
        self.NSSD = nl // 2


def param_layout(cfg):
    off = {}
    o = 0

    def put(name, n):
        nonlocal o
        off[name] = (o, n)
        o += n
    put("norm_mix", cfg.NL * DC)
    put("norm_ffn", cfg.NL * DC)
    put("norm_final", DC)
    put("sc_conv_w", cfg.NCONV * 3 * DC)
    put("ssd_conv_w", max(cfg.NSSD, 1) * 4 * 48)
    put("ssd_conv_b", max(cfg.NSSD, 1) * 48)
    put("ffn_conv_w", cfg.NL * 3 * 88)
    put("ffn_conv_b", cfg.NL * 88)
    put("ssd_norm_w", max(cfg.NSSD, 1) * 32)
    put("dt_bias", max(cfg.NSSD, 1) * 64)
    put("a_log", max(cfg.NSSD, 1) * 64)
    put("ssd_d", max(cfg.NSSD, 1) * 64)
    return off, o


class Prog:
    def __init__(self, cfg):
        self.cfg = cfg
        self.S = Sched()
        self.nc = bass.Bass("TRN2", target_bir_lowering=False)
        self.stack = contextlib.ExitStack()
        self.bank_rr = 0
        self.uid = 0

    def sb(self, name, shape, dt):
        return self.stack.enter_context(self.nc.sbuf_tensor("sb_" + name, shape, dt))

    def dram_in(self, name, shape, dt=F32):
        return self.nc.dram_tensor(name, list(shape), dt, kind="ExternalInput").ap()

    def dram_out(self, name, shape, dt=F32):
        return self.nc.dram_tensor(name, list(shape), dt, kind="ExternalOutput").ap()

    def bank(self):
        b = self.bank_rr % 8
        self.bank_rr += 1
        return b

    def arena_reset(self):
        self.S.barrier()
        self.a_off = 0

    def carve(self, words, dt=F32, shape=None):
        a = self.a_off
        self.a_off += words
        assert self.a_off <= self.AW, ("arena overflow", self.a_off, self.AW)
        v = self.arena[:, a:a + words]
        if dt == BF16:
            v = v.bitcast(BF16)
        return v

    def mm(self, out, lhsT, rhs, start, stop, reads, writes):
        self.S.add("pe", lambda e: e.matmul(out, lhsT, rhs, start=start, stop=stop),
                   reads=reads, writes=writes)

    def tr(self, out, in_, ident, reads, writes):
        self.S.add("pe", lambda e: e.transpose(out, in_, ident), reads=reads, writes=writes)

    def act(self, out, in_, func, reads, writes, bias=None, scale=None, accum_out=None):
        kw = {}
        if bias is not None:
            kw["bias"] = bias
        if scale is not None:
            kw["scale"] = scale
        if accum_out is not None:
            kw["accum_out"] = accum_out
        self.S.add("act", lambda e: e.activation(out, in_, func, **kw), reads=reads, writes=writes)

    def tt(self, out, in0, in1, op, reads, writes, eng="dve"):
        self.S.add(eng, lambda e: e.tensor_tensor(out, in0, in1, op), reads=reads, writes=writes)

    def ts(self, out, in0, s1, s2, op0, op1, reads, writes, eng="dve"):
        if op1 is None:
            self.S.add(eng, lambda e: e.tensor_scalar(out, in0, s1, None, op0), reads=reads, writes=writes)
        else:
            self.S.add(eng, lambda e: e.tensor_scalar(out, in0, s1, s2, op0, op1), reads=reads, writes=writes)

    def stt(self, out, in0, scalar, in1, op0, op1, reads, writes, eng="dve"):
        self.S.add(eng, lambda e: e.scalar_tensor_tensor(out, in0, scalar, in1, op0, op1),
                   reads=reads, writes=writes)

    def cp(self, out, in_, reads, writes, eng="dve"):
        if eng == "act":
            self.S.add("act", lambda e: e.copy(out, in_), reads=reads, writes=writes)
        else:
            self.S.add(eng, lambda e: e.tensor_copy(out, in_), reads=reads, writes=writes)

    def memset(self, ap, val, writes, eng="dve"):
        self.S.add(eng, lambda e: e.memset(ap, val), writes=writes)

    def dma(self, out, in_, reads, writes, q="sp", nobar=False):
        self.S.add(q, lambda e: e.dma_start(out=out, in_=in_), reads=reads, writes=writes,
                   dma=True, nobar=nobar)

    def slab_plan_layer(self, l):
        cfg = self.cfg
        out = []
        j = l // 2

        def src(w, r0, kc, c0, wd):
            return w[r0:r0 + kc * 128, c0:c0 + wd].rearrange("(kc p) m -> p kc m", p=128)
        if l % 2 == 0:
            w_in = self.w["sc_w_in"][j]
            for i in range(DC):
                out.append(("sc_in", 16, 384, [(0, src(w_in, 0, 16, i * 128, 128)),
                                               (128, src(w_in, 0, 16, D + i * 128, 128)),
                                               (256, src(w_in, 0, 16, 2 * D + i * 128, 128))]))
            w_out = self.w["sc_w_out"][j]
            for m in range(4):
                out.append(("sc_out", 16, 512, [(0, src(w_out, 0, 16, m * 512, 512))]))
        else:
            w_in = self.w["ssd_w_in"][j]
            out.append(("ssd_dt", 16, 512, [(0, src(w_in, 0, 16, INNER + CONVD + 64 - 512, 512))]))
            for s in range(4):
                out.append(("ssd_bc", 16, 512, [(0, src(w_in, 0, 16, INNER + INNER + s * 512, 512))]))
            w_out = self.w["ssd_w_out"][j]
            for g in range(GROUPS):
                out.append(("ssd_z", 16, 512, [(0, src(w_in, 0, 16, g * 512, 512))]))
                out.append(("ssd_x", 16, 512, [(0, src(w_in, 0, 16, INNER + g * 512, 512))]))
                out.append(("ssd_out", 4, 2048, [(0, src(w_out, g * 512, 4, 0, 2048))]))
        w_up = self.w["ffn_w_up"][l]
        for s in range(FC // 2):
            out.append(("ffn_up", 16, 512, [(0, src(w_up, 0, 16, s * 256, 256)),
                                            (256, src(w_up, 0, 16, DFF + s * 256, 256))]))
        w_dn = self.w["ffn_w_down"][l]
        for m in range(4):
            for (k0, kc) in ((0, 16), (16, 16), (32, 12)):
                out.append(("ffn_dn", kc, 512, [(0, src(w_dn, k0 * 128, kc, m * 512, 512))]))
        return out

    def slab_issue(self, idx):
        tag, kc, wd, parts = self.slab_seq[idx]
        b = idx % self.NSLAB
        view = self.slab[b][:, 0:kc * wd].rearrange("p (k m) -> p k m", m=wd)
        for (co, s) in parts:
            ncol = s.shape[2]
            self.dma(view[:, :, co:co + ncol], s, reads=(), writes=(("slab", b),), q="pool", nobar=True)

    def next_slab(self, tag):
        idx = self.slab_pos
        self.slab_pos += 1
        t, kc, wd, parts = self.slab_seq[idx]
        assert t == tag, (t, tag, idx)
        while self.slab_issued < min(len(self.slab_seq), idx + self.NSLAB):
            self.slab_issue(self.slab_issued)
            self.slab_issued += 1
        b = idx % self.NSLAB
        view = self.slab[b][:, 0:kc * wd].rearrange("p (k m) -> p k m", m=wd)
        return view, ("slab", b)

    def build(self):
        cfg = self.cfg
        nc = self.nc
        NL, NCONV, NSSD, NSEG = cfg.NL, cfg.NCONV, cfg.NSSD, cfg.NSEG
        self.xin = self.dram_in("xin", [cfg.NTOK, D])
        self.cst_d = self.dram_in("cst", [128, 512])
        poff, pn = param_layout(cfg)
        self.poff = poff
        self.prm_d = self.dram_in("prm", [128, pn])
        self.w = {
            "sc_w_in": self.dram_in("sc_w_in", [max(NCONV, 1), D, 3 * D]),
            "sc_w_out": self.dram_in("sc_w_out", [max(NCONV, 1), D, D]),
            "ssd_w_in": self.dram_in("ssd_w_in", [max(NSSD, 1), D, 2 * INNER + 2 * GROUPS * NSTATE + HEADS]),
            "ssd_w_out": self.dram_in("ssd_w_out", [max(NSSD, 1), INNER, D]),
            "ffn_w_up": self.dram_in("ffn_w_up", [NL, D, 2 * DFF]),
            "ffn_w_down": self.dram_in("ffn_w_down", [NL, DFF, D]),
        }
        self.st_ca = self.dram_in("st_ca", [max(NCONV, 1), NSEG, 2, D])
        self.st_sc = self.dram_in("st_sc", [max(NSSD, 1), NSEG, 3, CONVD])
        self.st_ss = self.dram_in("st_ss", [max(NSSD, 1), NSEG, HEADS, 64, NSTATE])
        self.st_ff = self.dram_in("st_ff", [NL, NSEG, 2, 2 * DFF])
        self.y_d = self.dram_out("y", [cfg.NTOK, D])
        self.o_ca = self.dram_out("o_ca", [max(NCONV, 1), 1 + NSEG, 2, D])
        self.o_sc = self.dram_out("o_sc", [max(NSSD, 1), 1 + NSEG, 3, CONVD])
        self.o_ss = self.dram_out("o_ss", [max(NSSD, 1), 1 + NSEG, HEADS, 64, NSTATE])
        self.o_ff = self.dram_out("o_ff", [NL, 1 + NSEG, 2, 2 * DFF])
        self.sstate = nc.dram_tensor("sstate", [max(NSSD, 1), GROUPS, 128, 512], F32, kind="Internal").ap()

        self.x = self.sb("x", [128, DC, 512], F32)
        self.h = self.sb("h", [128, DC, 512], BF16)
        self.NSLAB = 2
        self.slab = [self.sb("slab%d" % i, [128, 8192], BF16) for i in range(self.NSLAB)]
        self.io = self.sb("io", [128, 2048], F32)
        self.cst = self.sb("cst", [128, 512], F32)
        self.prm = self.sb("prm", [128, pn], F32)
        self.ident_b = self.sb("ident_b", [128, 128], BF16)
        self.ones_b = self.sb("ones_b", [128, 128], BF16)
        self.tri_b = self.sb("tri_b", [128, 128], BF16)
        self.nega = self.sb("nega", [128, max(NSSD, 1) * 64], F32)
        self.sq = [self.sb("sq%d" % i, [128, 512], BF16) for i in range(2)]
        self.rstd = self.sb("rstd", [128, 512], F32)
        self.hp_ff = self.sb("hp_ff", [128, NL, 2 * 88], F32)
        self.hp_sc = self.sb("hp_sc", [128, max(NSSD, 1), 3 * 48], F32)
        self.hp_ca = self.sb("hp_ca", [128, max(NCONV, 1), 2 * DC], F32)
        self.hs_ff = self.sb("hs_ff", [128, NSEG * 2 * 88], F32)
        self.hs_sc = self.sb("hs_sc", [128, NSEG * 3 * 48], F32)
        self.hs_ca = self.sb("hs_ca", [128, NSEG * 2 * DC], F32)
        self.hstg = [self.sb("hstg%d" % i, [128, 128], F32) for i in range(2)]
        self.hstg_rr = 0
        self.cb2 = self.sb("cb2", [128, 2], F32)
        self.epsb = self.cb2[:, 0:1]
        self.oneb = self.cb2[:, 1:2]
        self.AW = 22000
        self.arena = self.sb("arena", [128, self.AW], F32)
        self.ps = [self.stack.enter_context(nc.psum_tensor("ps%d" % i, [128, 512], F32)) for i in range(8)]

        self.ident_f = self.cst[:, 0:128]
        self.tri = self.cst[:, 128:256]
        self.maskneg = self.cst[:, 256:384]
        self.ones_f = self.cst[:, 384:512]

        tiles = []
        for k in range(cfg.NPT // cfg.TT):
            tiles.append(dict(kind="p", r0=k * cfg.TT, T=cfg.TT, nseg=1, L=cfg.TT, Q=cfg.QP,
                              first=(k == 0), last=(k == cfg.NPT // cfg.TT - 1)))
        if NSEG > 0:
            tiles.append(dict(kind="s", r0=cfg.NPT, T=NSEG * cfg.SL, nseg=NSEG, L=cfg.SL, Q=cfg.SL,
                              first=True, last=True))
        self.slab_seq = []
        for t in tiles:
            for l in range(NL):
                self.slab_seq += self.slab_plan_layer(l)
        self.slab_pos = 0
        self.slab_issued = 0

        self.a_off = 0
        self.dma(self.cst[:, :], self.cst_d, reads=(), writes=("cst",))
        self.dma(self.prm[:, :], self.prm_d, reads=(), writes=("prm",))
        self.cp(self.ident_b[:, :], self.ident_f, reads=("cst",), writes=("identb",))
        self.cp(self.ones_b[:, :], self.ones_f, reads=("cst",), writes=("onesb",))
        self.cp(self.tri_b[:, :], self.tri, reads=("cst",), writes=("trib",))
        if NSSD > 0:
            o, n = poff["a_log"]
            self.act(self.nega[:, :], self.prm[:, o:o + n], AF.Exp, reads=("prm",), writes=("nega",))
            self.ts(self.nega[:, :], self.nega[:, :], -1.0, None, ALU.mult, None, reads=("nega",), writes=("nega",))
        self.memset(self.hp_ff[:, :, :], 0.0, writes=("hff",))
        self.memset(self.hp_sc[:, :, :], 0.0, writes=("hsc",))
        self.memset(self.hp_ca[:, :, :], 0.0, writes=("hca",))
        self.memset(self.cb2[:, 0:1], EPS, writes=("cb2",))
        self.memset(self.cb2[:, 1:2], 1.0, writes=("cb2",))
        self.S.barrier()

        try:
            for t in tiles:
                self.tile = t
                self.run_tile(t)
        except StopBuild:
            pass

        self.S.emit(nc, self.stack)
        self.stack.close()
        return nc

    def P(self, name, idx, n):
        o, tot = self.poff[name]
        return self.prm[:, o + idx * n:o + (idx + 1) * n]

    def run_tile(self, t):
        cfg = self.cfg
        self.load_tile(t)
        for l in range(cfg.NL):
            self.rmsnorm(t, ("norm_mix", l))
            if l % 2 == 0:
                self.conv_mixer(t, l)
            else:
                self.ssd_mixer(t, l)
            self.rmsnorm(t, ("norm_ffn", l))
            self.ffn(t, l)
        self.final_out(t)

    def tok_groups(self, T):
        out = []
        o = 0
        while o < T:
            n = min(128, T - o)
            out.append((o, n))
            o += n
        return out

    def load_tile(self, t):
        T = t["T"]
        for (o, n) in self.tok_groups(T):
            self.dma(self.io[0:n, :], self.xin[t["r0"] + o:t["r0"] + o + n, :], reads=(), writes=("io",))
            for cq in range(4):
                b = self.bank()
                for j in range(4):
                    c = cq * 4 + j
                    self.tr(self.ps[b][:, j * 128:j * 128 + n], self.io[0:n, c * 128:(c + 1) * 128],
                            self.ident_f[0:n, 0:n], reads=("io", "cst"), writes=(("ps", b),))
                src = self.ps[b][:, :].rearrange("p (j m) -> p j m", m=128)[:, :, 0:n]
                dst = self.x[:, cq * 4:cq * 4 + 4, o:o + n]
                self.cp(dst, src, reads=(("ps", b),), writes=tuple(("x", cq * 4 + j) for j in range(4)),
                        eng=("act" if cq % 2 == 0 else "dve"))

    def rms_stats(self, t):
        T = t["T"]
        b = self.bank()
        for c in range(DC):
            s = self.sq[c % 2]
            self.act(s[:, 0:T], self.x[:, c, 0:T], AF.Square, reads=(("x", c),), writes=(("sq", c % 2),))
            self.mm(self.ps[b][:, 0:T], self.ones_b[:, :], s[:, 0:T], start=(c == 0), stop=(c == DC - 1),
                    reads=(("sq", c % 2), "onesb"), writes=(("ps", b),))
        self.act(self.rstd[:, 0:T], self.ps[b][:, 0:T], AF.Sqrt, reads=(("ps", b),), writes=("rstd",),
                 bias=self.epsb[:, 0:1], scale=1.0 / D)
        self.S.add("dve", lambda e: e.reciprocal(self.rstd[:, 0:T], self.rstd[:, 0:T]),
                   reads=("rstd",), writes=("rstd",))

    def rmsnorm(self, t, which):
        T = t["T"]
        name, l = which
        self.rms_stats(t)
        w = self.P(name, l, DC)
        for c in range(DC):
            self.stt(self.h[:, c, 0:T], self.x[:, c, 0:T], w[:, c:c + 1], self.rstd[:, 0:T], ALU.mult, ALU.mult,
                     reads=(("x", c), "rstd", "prm"), writes=(("h", c),))

    def final_out(self, t):
        T = t["T"]
        self.arena_reset()
        yq = [self.carve(512).rearrange("p (j m) -> p j m", m=128) for _ in range(2)]
        self.rms_stats(t)
        w = self.P("norm_final", 0, DC)
        qi = 0
        for (o, n) in self.tok_groups(T):
            for cq in range(4):
                q = yq[qi % 2]
                rq = ("yq", qi % 2)
                qi += 1
                for j in range(4):
                    c = cq * 4 + j
                    self.stt(q[:, j, 0:n], self.x[:, c, o:o + n], w[:, c:c + 1], self.rstd[:, o:o + n],
                             ALU.mult, ALU.mult, reads=(("x", c), "rstd", "prm"), writes=(rq,))
                b = self.bank()
                for j in range(4):
                    self.tr(self.ps[b][0:n, j * 128:(j + 1) * 128], q[:, j, 0:n], self.ident_f,
                            reads=(rq, "cst"), writes=(("ps", b),))
                self.cp(self.io[0:n, cq * 512:(cq + 1) * 512], self.ps[b][0:n, :], reads=(("ps", b),),
                        writes=("io",), eng="act")
            self.dma(self.y_d[t["r0"] + o:t["r0"] + o + n, :], self.io[0:n, :], reads=("io",), writes=())

    def halo_view(self, t, kind, l, taps, nch):
        hw = taps - 1
        if t["kind"] == "p":
            buf = {"ff": self.hp_ff, "sc": self.hp_sc, "ca": self.hp_ca}[kind]
            li = l if kind == "ff" else l // 2
            return buf[:, li:li + 1, :].rearrange("p s (t c) -> p s t c", c=nch), (("h" + kind),)
        buf = {"ff": self.hs_ff, "sc": self.hs_sc, "ca": self.hs_ca}[kind]
        return buf[:, :].rearrange("p (s t c) -> p s t c", t=hw, c=nch), (("hs" + kind),)

    def halo_load_sample(self, t, kind, l, taps, nch):
        if t["kind"] != "s":
            return
        hw = taps - 1
        nseg = t["nseg"]
        src = {"ff": self.st_ff, "sc": self.st_sc, "ca": self.st_ca}[kind]
        li = l if kind == "ff" else l // 2
        rows = src[li].rearrange("s t (c p) -> (s t c) p", p=128)
        buf = {"ff": self.hs_ff, "sc": self.hs_sc, "ca": self.hs_ca}[kind]
        res = ("hs" + kind)
        nrows = nseg * hw * nch
        r = 0
        while r < nrows:
            n = min(128, nrows - r)
            st = self.hstg[self.hstg_rr % 2]
            rs = ("hstg", self.hstg_rr % 2)
            self.hstg_rr += 1
            self.dma(st[0:n, :], rows[r:r + n, :], reads=(), writes=(rs,))
            b = self.bank()
            self.tr(self.ps[b][:, 0:n], st[0:n, :], self.ident_f[0:n, 0:n], reads=(rs, "cst"), writes=(("ps", b),))
            self.cp(buf[:, r:r + n], self.ps[b][:, 0:n], reads=(("ps", b),),
                    writes=tuple((res, ci) for ci in range(nch)), eng="act")
            r += n

    def halo_store(self, t, kind, l, taps, nch):
        if not t["last"]:
            return
        hw = taps - 1
        li = l if kind == "ff" else l // 2
        dst_t = {"ff": self.o_ff, "sc": self.o_sc, "ca": self.o_ca}[kind]
        if t["kind"] == "p":
            buf = {"ff": self.hp_ff, "sc": self.hp_sc, "ca": self.hp_ca}[kind][:, li, :]
            res = ("h" + kind)
            rows = dst_t[li, 0:1].rearrange("s t (c p) -> (s t c) p", p=128)
            nrows = hw * nch
        else:
            buf = {"ff": self.hs_ff, "sc": self.hs_sc, "ca": self.hs_ca}[kind][:, :]
            res = ("hs" + kind)
            rows = dst_t[li, 1:1 + t["nseg"]].rearrange("s t (c p) -> (s t c) p", p=128)
            nrows = t["nseg"] * hw * nch
        r = 0
        while r < nrows:
            n = min(128, nrows - r)
            st = self.hstg[self.hstg_rr % 2]
            rs = ("hstg", self.hstg_rr % 2)
            self.hstg_rr += 1
            b = self.bank()
            self.tr(self.ps[b][0:n, 0:128], buf[:, r:r + n], self.ident_f,
                    reads=tuple((res, ci) for ci in range(nch)) + ("cst",), writes=(("ps", b),))
            self.cp(st[0:n, :], self.ps[b][0:n, 0:128], reads=(("ps", b),), writes=(rs,), eng="act")
            self.dma(rows[r:r + n, :], st[0:n, :], reads=(rs,), writes=())
            r += n

    def conv_chunk(self, t, src_ap, src_res, inb, inb_res, acc, acc_res, halo, halo_res, ci, taps,
                   wts, bias, evac_eng="act", src_is_sbuf_tt=None):
        nseg, L, T = t["nseg"], t["L"], t["T"]
        hw = taps - 1
        Pp = hw + L
        TP = nseg * Pp
        inv = inb[:, 0:TP].rearrange("p (s q) -> p s q", q=Pp)
        accv = acc[:, 0:TP].rearrange("p (s q) -> p s q", q=Pp)
        srcv = src_ap.rearrange("p (s q) -> p s q", q=L)
        if src_is_sbuf_tt is None:
            self.cp(inv[:, :, hw:], srcv, reads=(src_res,), writes=(inb_res,), eng=evac_eng)
        else:
            other = src_is_sbuf_tt[0].rearrange("p (s q) -> p s q", q=L)
            self.tt(inv[:, :, hw:], other, srcv, ALU.mult, reads=(src_res, src_is_sbuf_tt[1]), writes=(inb_res,))
        self.cp(inv[:, :, 0:hw], halo[:, :, :, ci], reads=(halo_res + (ci,),), writes=(inb_res,), eng="act")
        if bias is not None:
            self.ts(acc[:, hw:TP], inb[:, hw:TP], wts[taps - 1], bias, ALU.mult, ALU.add,
                    reads=(inb_res, "prm"), writes=(acc_res,))
        else:
            self.ts(acc[:, hw:TP], inb[:, hw:TP], wts[taps - 1], None, ALU.mult, None,
                    reads=(inb_res, "prm"), writes=(acc_res,))
        for k in range(1, taps):
            self.stt(acc[:, hw:TP], inb[:, hw - k:TP - k], wts[taps - 1 - k], acc[:, hw:TP], ALU.mult, ALU.add,
                     reads=(inb_res, acc_res, "prm"), writes=(acc_res,))
        self.cp(halo[:, :, :, ci], inv[:, :, L:L + hw], reads=(inb_res,), writes=(halo_res + (ci,),), eng="act")
        return accv[:, :, hw:]

    def conv_mixer(self, t, l):
        T, nseg, L = t["T"], t["nseg"], t["L"]
        j = l // 2
        self.arena_reset()
        y = self.carve(DC * 256, BF16).rearrange("p (c m) -> p c m", m=512)
        tmpc = [self.carve(512) for _ in range(2)]
        inb = [self.carve(544) for _ in range(2)]
        acc = [self.carve(544) for _ in range(2)]
        self.halo_load_sample(t, "ca", l, 3, DC)
        halo, hres = self.halo_view(t, "ca", l, 3, DC)
        cw = self.P("sc_conv_w", j, 3 * DC)
        for i in range(DC):
            sl, sres = self.next_slab("sc_in")
            bb, bc, bv = self.bank(), self.bank(), self.bank()
            for (bk, co) in ((bc, 128), (bv, 256), (bb, 0)):
                for kc in range(DC):
                    self.mm(self.ps[bk][:, 0:T], sl[:, kc, co:co + 128], self.h[:, kc, 0:T],
                            start=(kc == 0), stop=(kc == DC - 1), reads=(sres, ("h", kc)), writes=(("ps", bk),))
            k = i % 2
            self.cp(tmpc[k][:, 0:T], self.ps[bc][:, 0:T], reads=(("ps", bc),), writes=(("tmpc", k),), eng="act")
            wts = [cw[:, tp * DC + i:tp * DC + i + 1] for tp in range(3)]
            res = self.conv_chunk(t, self.ps[bv][:, 0:T], ("ps", bv), inb[k], ("inb", k), acc[k], ("acc", k),
                                  halo, hres, i, 3, wts, None, src_is_sbuf_tt=(tmpc[k][:, 0:T], ("tmpc", k)))
            yv = y[:, i, 0:T].rearrange("p (s q) -> p s q", q=L)
            bsrc = self.ps[bb][:, 0:T].rearrange("p (s q) -> p s q", q=L)
            self.tt(yv, res, bsrc, ALU.mult, reads=(("acc", k), ("ps", bb)), writes=(("y", i),))
        for m in range(4):
            sl, sres = self.next_slab("sc_out")
            for mc in range(4):
                b = self.bank()
                c = m * 4 + mc
                for kc in range(DC):
                    self.mm(self.ps[b][:, 0:T], sl[:, kc, mc * 128:(mc + 1) * 128], y[:, kc, 0:T],
                            start=(kc == 0), stop=(kc == DC - 1), reads=(sres, ("y", kc)), writes=(("ps", b),))
                self.tt(self.x[:, c, 0:T], self.x[:, c, 0:T], self.ps[b][:, 0:T], ALU.add,
                        reads=(("x", c), ("ps", b)), writes=(("x", c),))
        self.halo_store(t, "ca", l, 3, DC)

    def ffn(self, t, l):
        T, nseg, L = t["T"], t["nseg"], t["L"]
        self.arena_reset()
        g = self.carve(FC * 256, BF16).rearrange("p (c m) -> p c m", m=512)
        ina = [self.carve(544) for _ in range(2)]
        inv_ = [self.carve(544) for _ in range(2)]
        acca = [self.carve(544) for _ in range(2)]
        accv = [self.carve(544) for _ in range(2)]
        sa = [self.carve(544) for _ in range(2)]
        self.halo_load_sample(t, "ff", l, 3, 88)
        halo, hres = self.halo_view(t, "ff", l, 3, 88)
        cw = self.P("ffn_conv_w", l, 3 * 88)
        cb = self.P("ffn_conv_b", l, 88)
        hw = 2
        Pp = hw + L
        TP = nseg * Pp
        for s in range(FC // 2):
            sl, sres = self.next_slab("ffn_up")
            for q2 in range(2):
                q = s * 2 + q2
                k = q % 2
                ba, bv = self.bank(), self.bank()
                for (bk, co) in ((ba, q2 * 128), (bv, 256 + q2 * 128)):
                    for kc in range(DC):
                        self.mm(self.ps[bk][:, 0:T], sl[:, kc, co:co + 128], self.h[:, kc, 0:T],
                                start=(kc == 0), stop=(kc == DC - 1), reads=(sres, ("h", kc)), writes=(("ps", bk),))
                wa = [cw[:, tp * 88 + q:tp * 88 + q + 1] for tp in range(3)]
                wv = [cw[:, tp * 88 + FC + q:tp * 88 + FC + q + 1] for tp in range(3)]
                ra = self.conv_chunk(t, self.ps[ba][:, 0:T], ("ps", ba), ina[k], ("ina", k), acca[k], ("acca", k),
                                     halo, hres, q, 3, wa, cb[:, q:q + 1])
                rv = self.conv_chunk(t, self.ps[bv][:, 0:T], ("ps", bv), inv_[k], ("inv", k), accv[k], ("accv", k),
                                     halo, hres, FC + q, 3, wv, cb[:, FC + q:FC + q + 1])
                self.act(sa[k][:, hw:TP], acca[k][:, hw:TP], AF.Silu, reads=(("acca", k),), writes=(("sa", k),))
                sav = sa[k][:, 0:TP].rearrange("p (s q) -> p s q", q=Pp)[:, :, hw:]
                gv = g[:, q, 0:T].rearrange("p (s q) -> p s q", q=L)
                self.tt(gv, sav, rv, ALU.mult, reads=(("sa", k), ("accv", k)), writes=(("g", q),))
        for m in range(4):
            bs = [self.bank() for _ in range(4)]
            for (k0, kcn) in ((0, 16), (16, 16), (32, 12)):
                sl, sres = self.next_slab("ffn_dn")
                for mc in range(4):
                    for kc in range(kcn):
                        kk = k0 + kc
                        self.mm(self.ps[bs[mc]][:, 0:T], sl[:, kc, mc * 128:(mc + 1) * 128], g[:, kk, 0:T],
                                start=(kk == 0), stop=(kk == FC - 1), reads=(sres, ("g", kk)),
                                writes=(("ps", bs[mc]),))
            for mc in range(4):
                c = m * 4 + mc
                self.tt(self.x[:, c, 0:T], self.x[:, c, 0:T], self.ps[bs[mc]][:, 0:T], ALU.add,
                        reads=(("x", c), ("ps", bs[mc])), writes=(("x", c),))
        self.halo_store(t, "ff", l, 3, 88)

    def ssd_mixer(self, t, l):
        cfg = self.cfg
        T, nseg, L, Q = t["T"], t["nseg"], t["L"], t["Q"]
        j = l // 2
        nck = T // Q
        cps = L // Q
        self.arena_reset()
        zs = self.carve(4 * 256, BF16).rearrange("p (c m) -> p c m", m=512)
        xtm = self.carve(4 * 256, BF16).rearrange("p (c m) -> p c m", m=512)
        gfm = self.carve(4 * 256, BF16).rearrange("p (c m) -> p c m", m=512)
        bfm = self.carve(8 * 256, BF16).rearrange("p (c m) -> p c m", m=512)
        cfm = self.carve(8 * 256, BF16).rearrange("p (c m) -> p c m", m=512)
        btm = self.carve(4 * 512, BF16).rearrange("p (c m) -> p c m", m=1024)
        def dtbuf():
            return self.carve(4 * 64).rearrange("p (c m) -> p c m", m=64)
        dt_a, dta_a, cs_a, ecs_a, wt_a = dtbuf(), dtbuf(), dtbuf(), dtbuf(), dtbuf()
        el_a = dtbuf()
        d3 = self.carve(4 * 3 * 32, BF16).rearrange("p (c s m) -> p c s m", s=3, m=64)
        tmp64 = [self.carve(64) for _ in range(3)]
        inb = [self.carve(544) for _ in range(2)]
        acc = [self.carve(544) for _ in range(2)]
        xc = [self.carve(256, BF16) for _ in range(2)]
        cbg = [self.carve(128) for _ in range(2)]
        seg = [self.carve(1024).rearrange("p (h m) -> p h m", m=128) for _ in range(2)]
        Mb = [self.carve(512, BF16).rearrange("p (h m) -> p h m", m=128) for _ in range(2)]
        xw = [self.carve(256, BF16) for _ in range(2)]
        yb = [self.carve(512) for _ in range(2)]
        ytmp = self.carve(512)
        ss = [self.carve(2) for _ in range(2)]
        junk = self.carve(512)
        Sf = [self.carve(512) for _ in range(2)]
        Sb = [self.carve(256, BF16) for _ in range(2)]
        stg = self.carve(512).rearrange("p (b m) -> p b m", m=128)

        self.halo_load_sample(t, "sc", l, 4, 48)
        halo, hres = self.halo_view(t, "sc", l, 4, 48)
        cw = self.P("ssd_conv_w", j, 4 * 48)
        cb = self.P("ssd_conv_b", j, 48)
        dtb = self.P("dt_bias", j, 64)
        dsk = self.P("ssd_d", j, 64)
        nw = self.P("ssd_norm_w", j, 32)
        nega = self.nega[:, j * 64:(j + 1) * 64]

        def tok(c):
            return slice(c * Q, (c + 1) * Q)

        sl, sres = self.next_slab("ssd_dt")
        for c in range(nck):
            b = self.bank()
            for kc in range(DC):
                self.mm(self.ps[b][0:Q, 0:64], self.h[:, kc, tok(c)], sl[:, kc, 448:512], start=(kc == 0),
                        stop=(kc == DC - 1), reads=(sres, ("h", kc)), writes=(("ps", b),))
            xd, ax, ee = tmp64[0], tmp64[1], tmp64[2]
            self.tt(xd[0:Q, :], self.ps[b][0:Q, 0:64], dtb[0:Q, :], ALU.add, reads=(("ps", b), "prm"), writes=("xd",))
            self.act(ax[0:Q, :], xd[0:Q, :], AF.Abs, reads=("xd",), writes=("ax",))
            self.act(ee[0:Q, :], ax[0:Q, :], AF.Exp, reads=("ax",), writes=("ee",), scale=-1.0)
            self.act(ee[0:Q, :], ee[0:Q, :], AF.Ln, reads=("ee",), writes=("ee",), bias=self.oneb[0:Q, 0:1])
            self.stt(dt_a[0:Q, c, :], xd[0:Q, :], 0.0, ee[0:Q, :], ALU.max, ALU.add, reads=("xd", "ee"),
                     writes=(("dt", c),))
            self.tt(dta_a[0:Q, c, :], dt_a[0:Q, c, :], nega[0:Q, :], ALU.mult, reads=(("dt", c), "nega"),
                    writes=(("dta", c),))
            r1, r2 = tmp64[1], tmp64[2]
            self.cp(d3[0:Q, c, 0, :], dta_a[0:Q, c, :], reads=(("dta", c),), writes=(("d3", c),))
            self.tt(r1[0:Q, :], dta_a[0:Q, c, :], d3[0:Q, c, 0, :], ALU.subtract, reads=(("dta", c), ("d3", c)),
                    writes=("ax",))
            self.cp(d3[0:Q, c, 1, :], r1[0:Q, :], reads=("ax",), writes=(("d3", c),))
            self.tt(r2[0:Q, :], r1[0:Q, :], d3[0:Q, c, 1, :], ALU.subtract, reads=("ax", ("d3", c)), writes=("ee",))
            self.cp(d3[0:Q, c, 2, :], r2[0:Q, :], reads=("ee",), writes=(("d3", c),))
            b1, b2 = self.bank(), self.bank()
            for s3 in range(3):
                self.mm(self.ps[b1][0:Q, 0:64], self.tri_b[0:Q, 0:Q], d3[0:Q, c, s3, :], start=(s3 == 0), stop=(s3 == 2),
                        reads=("trib", ("d3", c)), writes=(("ps", b1),))
            for s3 in range(3):
                self.mm(self.ps[b2][:, 0:64], self.ones_b[0:Q, :], d3[0:Q, c, s3, :], start=(s3 == 0), stop=(s3 == 2),
                        reads=("onesb", ("d3", c)), writes=(("ps", b2),))
            self.cp(cs_a[0:Q, c, :], self.ps[b1][0:Q, 0:64], reads=(("ps", b1),), writes=(("cs", c),))
            self.act(ecs_a[0:Q, c, :], self.ps[b1][0:Q, 0:64], AF.Exp, reads=(("ps", b1),), writes=(("ecs", c),))
            self.act(el_a[:, c, :], self.ps[b2][:, 0:64], AF.Exp, reads=(("ps", b2),), writes=(("el", c),))
            self.tt(wt_a[0:Q, c, :], self.ps[b2][0:Q, 0:64], cs_a[0:Q, c, :], ALU.subtract,
                    reads=(("ps", b2), ("cs", c)), writes=(("wt", c),))
            self.act(wt_a[0:Q, c, :], wt_a[0:Q, c, :], AF.Exp, reads=(("wt", c),), writes=(("wt", c),))
            self.tt(wt_a[0:Q, c, :], wt_a[0:Q, c, :], dt_a[0:Q, c, :], ALU.mult, reads=(("wt", c), ("dt", c)),
                    writes=(("wt", c),))

        if _STOP == "dt":
            raise StopBuild()
        def xbc_chunk(sl, sres, col, ci, dst_ap, dst_res, k):
            b = self.bank()
            for kc in range(DC):
                self.mm(self.ps[b][:, 0:T], sl[:, kc, col:col + 128], self.h[:, kc, 0:T], start=(kc == 0),
                        stop=(kc == DC - 1), reads=(sres, ("h", kc)), writes=(("ps", b),))
            wts = [cw[:, tp * 48 + ci:tp * 48 + ci + 1] for tp in range(4)]
            res = self.conv_chunk(t, self.ps[b][:, 0:T], ("ps", b), inb[k], ("inb", k), acc[k], ("acc", k),
                                  halo, hres, ci, 4, wts, cb[:, ci:ci + 1])
            dv = dst_ap.rearrange("p (s q) -> p s q", q=L)
            self.act(dv, res, AF.Silu, reads=(("acc", k),), writes=(dst_res,))

        kk = 0
        for s4 in range(4):
            sl, sres = self.next_slab("ssd_bc")
            for mc in range(4):
                gi = (s4 % 2) * 4 + mc
                if s4 < 2:
                    xbc_chunk(sl, sres, mc * 128, 32 + gi, bfm[:, gi, 0:T], ("bfm", gi), kk % 2)
                    for c in range(nck):
                        b = self.bank()
                        pb = self.ps[b][:, :].bitcast(BF16)
                        self.tr(pb[0:Q, 0:128], bfm[:, gi, tok(c)], self.ident_b[:, :], reads=(("bfm", gi), "identb"),
                                writes=(("ps", b),))
                        self.cp(btm[0:Q, c, gi * 128:(gi + 1) * 128], pb[0:Q, 0:128], reads=(("ps", b),),
                                writes=(("btm", c, gi),), eng="act")
                else:
                    xbc_chunk(sl, sres, mc * 128, 40 + gi, cfm[:, gi, 0:T], ("cfm", gi), kk % 2)
                kk += 1

        if _STOP == "bc":
            raise StopBuild()
        for g in range(GROUPS):
            sl, sres = self.next_slab("ssd_z")
            for c in range(nck):
                b = self.bank()
                for kc in range(DC):
                    self.mm(self.ps[b][0:Q, 0:512], self.h[:, kc, tok(c)], sl[:, kc, 0:512], start=(kc == 0),
                            stop=(kc == DC - 1), reads=(sres, ("h", kc)), writes=(("ps", b),))
                self.act(zs[0:Q, c, :], self.ps[b][0:Q, 0:512], AF.Silu, reads=(("ps", b),), writes=(("zs", c),))
            sl, sres = self.next_slab("ssd_x")
            for cc in range(4):
                k = kk % 2
                kk += 1
                xbc_chunk(sl, sres, cc * 128, g * 4 + cc, xc[k][:, 0:T], ("xc", k), k)
                for c in range(nck):
                    b = self.bank()
                    pb = self.ps[b][:, :].bitcast(BF16)
                    self.tr(pb[0:Q, 0:128], xc[k][:, tok(c)], self.ident_b[:, :], reads=(("xc", k), "identb"),
                            writes=(("ps", b),))
                    self.cp(xtm[0:Q, c, cc * 128:(cc + 1) * 128], pb[0:Q, 0:128], reads=(("ps", b),),
                            writes=(("xtm", c),), eng="act")
            if _STOP == "zx":
                raise StopBuild()
            for c in range(nck):
                sg = c // cps
                first = (c % cps == 0)
                last = (c % cps == cps - 1)
                sk = (sg % 2) if t["kind"] == "s" else 0
                S_, Sb_ = Sf[sk], Sb[sk]
                rS, rSb = ("S", sk), ("Sb", sk)
                if first:
                    if t["kind"] == "p" and t["first"]:
                        self.memset(S_[:, :], 0.0, writes=(rS,))
                        self.memset(Sb_[:, :], 0.0, writes=(rSb,))
                    elif t["kind"] == "p":
                        self.dma(S_[:, :], self.sstate[j, g], reads=("sstate",), writes=(rS,))
                        self.cp(Sb_[:, :], S_[:, :], reads=(rS,), writes=(rSb,), eng="act")
                    else:
                        srcst = self.st_ss[j, sg, g * 8:(g + 1) * 8].rearrange("h p n -> (h p) n") \
                            .rearrange("(b r) n -> r b n", r=128)
                        self.dma(stg[:, :, :], srcst, reads=(), writes=("stg",))
                        b = self.bank()
                        for bl in range(4):
                            self.tr(self.ps[b][:, bl * 128:(bl + 1) * 128], stg[:, bl, :], self.ident_f,
                                    reads=("stg", "cst"), writes=(("ps", b),))
                        self.cp(S_[:, :], self.ps[b][:, :], reads=(("ps", b),), writes=(rS,))
                        self.cp(Sb_[:, :], self.ps[b][:, :], reads=(("ps", b),), writes=(rSb,), eng="act")
                k = (g * nck + c) % 2
                b = self.bank()
                self.mm(self.ps[b][0:Q, 0:Q], bfm[:, g, tok(c)], cfm[:, g, tok(c)], start=True, stop=True,
                        reads=(("bfm", g), ("cfm", g)), writes=(("ps", b),))
                self.cp(cbg[k][0:Q, 0:Q], self.ps[b][0:Q, 0:Q], reads=(("ps", b),), writes=(("cbg", k),), eng="act")
                for hh in range(2):
                    b = self.bank()
                    for h4 in range(4):
                        hl = hh * 4 + h4
                        hd = g * 8 + hl
                        for s3 in range(3):
                            lhs = d3[0:Q, c, s3, hd:hd + 1].broadcast_to([Q, Q])
                            self.mm(self.ps[b][0:Q, h4 * 128:h4 * 128 + Q], lhs, self.tri_b[0:Q, 0:Q], start=(s3 == 0),
                                    stop=(s3 == 2), reads=(("d3", c), "trib"), writes=(("ps", b),))
                    for h4 in range(4):
                        hl = hh * 4 + h4
                        hd = g * 8 + hl
                        self.stt(seg[k][0:Q, hl, 0:Q], self.ps[b][0:Q, h4 * 128:h4 * 128 + Q],
                                 cs_a[0:Q, c, hd:hd + 1], self.maskneg[0:Q, 0:Q], ALU.subtract, ALU.add,
                                 reads=(("ps", b), ("cs", c), "cst"), writes=(("seg", k),))
                self.act(seg[k][0:Q, :, 0:Q], seg[k][0:Q, :, 0:Q], AF.Exp, reads=(("seg", k),), writes=(("seg", k),))
                for hl in range(8):
                    hd = g * 8 + hl
                    self.stt(Mb[k][0:Q, hl, 0:Q], seg[k][0:Q, hl, 0:Q], dt_a[0:Q, c, hd:hd + 1], cbg[k][0:Q, 0:Q],
                             ALU.mult, ALU.mult, reads=(("seg", k), ("dt", c), ("cbg", k)), writes=(("M", k),))
                bi, ba = self.bank(), self.bank()
                self.mm(self.ps[bi][0:Q, 0:512], cfm[:, g, tok(c)], Sb_[:, :], start=True, stop=True,
                        reads=(("cfm", g), rSb), writes=(("ps", bi),))
                for hl in range(8):
                    self.mm(self.ps[ba][0:Q, hl * 64:(hl + 1) * 64], Mb[k][0:Q, hl, 0:Q], xtm[0:Q, c, hl * 64:(hl + 1) * 64],
                            start=True, stop=True, reads=(("M", k), ("xtm", c)), writes=(("ps", ba),))
                yv = yb[k][0:Q, :].rearrange("p (h m) -> p h m", m=64)
                ecsb = ecs_a[0:Q, c, g * 8:(g + 1) * 8].unsqueeze(2).broadcast_to([Q, 8, 64])
                self.tt(yv, self.ps[bi][0:Q, 0:512].rearrange("p (h m) -> p h m", m=64), ecsb, ALU.mult,
                        reads=(("ps", bi), ("ecs", c)), writes=(("yb", k),))
                self.tt(yb[k][0:Q, :], yb[k][0:Q, :], self.ps[ba][0:Q, 0:512], ALU.add,
                        reads=(("yb", k), ("ps", ba)), writes=(("yb", k),))
                dskb = dsk[0:Q, g * 8:(g + 1) * 8].unsqueeze(2).broadcast_to([Q, 8, 64])
                self.tt(ytmp[0:Q, :].rearrange("p (h m) -> p h m", m=64),
                        xtm[0:Q, c, :].rearrange("p (h m) -> p h m", m=64), dskb, ALU.mult,
                        reads=(("xtm", c), "prm"), writes=("ytmp",))
                self.tt(yb[k][0:Q, :], yb[k][0:Q, :], ytmp[0:Q, :], ALU.add, reads=(("yb", k), "ytmp"),
                        writes=(("yb", k),))
                self.tt(yb[k][0:Q, :], yb[k][0:Q, :], zs[0:Q, c, :], ALU.mult, reads=(("yb", k), ("zs", c)),
                        writes=(("yb", k),))
                self.act(junk[0:Q, :], yb[k][0:Q, :], AF.Square, reads=(("yb", k),), writes=("junk", ("ss", k)),
                         accum_out=ss[k][0:Q, 0:1])
                self.act(ss[k][0:Q, 0:1], ss[k][0:Q, 0:1], AF.Sqrt, reads=(("ss", k),), writes=(("ss", k),),
                         bias=self.epsb[0:Q, 0:1], scale=1.0 / 512.0)
                self.S.add("dve", lambda e, k=k: e.reciprocal(ss[k][0:Q, 0:1], ss[k][0:Q, 0:1]),
                           reads=(("ss", k),), writes=(("ss", k),))
                self.ts(zs[0:Q, c, :], yb[k][0:Q, :], ss[k][0:Q, 0:1], None, ALU.mult, None,
                        reads=(("yb", k), ("ss", k)), writes=(("zs", c),))
                wtb = wt_a[0:Q, c, g * 8:(g + 1) * 8].unsqueeze(2).broadcast_to([Q, 8, 64])
                self.tt(xw[k][0:Q, :].rearrange("p (h m) -> p h m", m=64),
                        xtm[0:Q, c, :].rearrange("p (h m) -> p h m", m=64), wtb, ALU.mult,
                        reads=(("xtm", c), ("wt", c)), writes=(("xw", k),))
                bu = self.bank()
                self.mm(self.ps[bu][:, 0:512], btm[0:Q, c, g * 128:(g + 1) * 128], xw[k][0:Q, :], start=True, stop=True,
                        reads=(("btm", c, g), ("xw", k)), writes=(("ps", bu),))
                elb = el_a[:, c, g * 8:(g + 1) * 8].unsqueeze(2).broadcast_to([128, 8, 64])
                self.tt(S_[:, :].rearrange("p (h m) -> p h m", m=64), S_[:, :].rearrange("p (h m) -> p h m", m=64),
                        elb, ALU.mult, reads=(rS, ("el", c)), writes=(rS,))
                self.tt(S_[:, :], S_[:, :], self.ps[bu][:, 0:512], ALU.add, reads=(rS, ("ps", bu)), writes=(rS,))
                if not last:
                    self.cp(Sb_[:, :], S_[:, :], reads=(rS,), writes=(rSb,), eng="act")
                if last:
                    if t["kind"] == "p" and not t["last"]:
                        self.dma(self.sstate[j, g], S_[:, :], reads=(rS,), writes=("sstate",))
                    else:
                        slot = 0 if t["kind"] == "p" else 1 + sg
                        dstst = self.o_ss[j, slot, g * 8:(g + 1) * 8].rearrange("h p n -> (h p) n") \
                            .rearrange("(b r) n -> r b n", r=128)
                        b = self.bank()
                        for bl in range(4):
                            self.tr(self.ps[b][:, bl * 128:(bl + 1) * 128], S_[:, bl * 128:(bl + 1) * 128],
                                    self.ident_f, reads=(rS, "cst"), writes=(("ps", b),))
                        self.cp(stg[:, :, :], self.ps[b][:, :].rearrange("p (b m) -> p b m", m=128),
                                reads=(("ps", b),), writes=("stg",), eng="act")
                        self.dma(dstst, stg[:, :, :], reads=("stg",), writes=())
            if _STOP == "chunk":
                raise StopBuild()
            for cc in range(4):
                for c in range(nck):
                    b = self.bank()
                    pb = self.ps[b][:, :].bitcast(BF16)
                    self.tr(pb[:, 0:Q], zs[0:Q, c, cc * 128:(cc + 1) * 128], self.ident_b[0:Q, 0:Q],
                            reads=(("zs", c), "identb"), writes=(("ps", b),))
                    self.act(gfm[:, cc, tok(c)], pb[:, 0:Q], AF.Copy, reads=(("ps", b), "prm"),
                             writes=(("gfm", cc),), scale=nw[:, g * 4 + cc:g * 4 + cc + 1])
            sl, sres = self.next_slab("ssd_out")
            for m in range(DC):
                b = self.bank()
                for kc in range(4):
                    self.mm(self.ps[b][:, 0:T], sl[:, kc, m * 128:(m + 1) * 128], gfm[:, kc, 0:T], start=(kc == 0),
                            stop=(kc == 3), reads=(sres, ("gfm", kc)), writes=(("ps", b),))
                self.tt(self.x[:, m, 0:T], self.x[:, m, 0:T], self.ps[b][:, 0:T], ALU.add,
                        reads=(("x", m), ("ps", b)), writes=(("x", m),))
            if _STOP == "g0":
                raise StopBuild()
        self.halo_store(t, "sc", l, 4, 48)


def build_program(cfg):
    p = Prog(cfg)
    poff, pn = param_layout(cfg)
    p.build_pre = None
    return p


def make_consts():
    c = np.zeros((128, 512), np.float32)
    c[:, 0:128] = np.eye(128, dtype=np.float32)
    c[:, 128:256] = np.triu(np.ones((128, 128), np.float32))
    c[:, 256:384] = np.where(np.arange(128)[:, None] <= np.arange(128)[None, :], 0.0, NEG)
    c[:, 384:512] = 1.0
    return c


def fm(v, nch):
    v = np.asarray(v, np.float32)
    lead = v.shape[:-1]
    return np.moveaxis(v.reshape(lead + (nch, 128)), -1, 0)


def pack_params(cfg, inp):
    poff, pn = param_layout(cfg)
    prm = np.zeros((128, pn), np.float32)

    def put(name, arr):
        o, n = poff[name]
        a = np.ascontiguousarray(arr, dtype=np.float32).reshape(128, -1)
        assert a.shape[1] == n, (name, a.shape, n)
        prm[:, o:o + n] = a
    NL, NCONV, NSSD = cfg.NL, cfg.NCONV, cfg.NSSD
    put("norm_mix", fm(inp["norm_mix"][:NL], DC))
    put("norm_ffn", fm(inp["norm_ffn"][:NL], DC))
    put("norm_final", fm(inp["norm_final"], DC))
    if NCONV:
        put("sc_conv_w", fm(inp["sc_conv_w"][:NCONV], DC))
    if NSSD:
        put("ssd_conv_w", fm(inp["ssd_conv_w"][:NSSD], 48))
        put("ssd_conv_b", fm(inp["ssd_conv_b"][:NSSD], 48))
        put("ssd_norm_w", fm(inp["ssd_norm_w"][:NSSD], 32))
        for nm, key in (("dt_bias", "ssd_dt_bias"), ("a_log", "ssd_a_log"), ("ssd_d", "ssd_d")):
            put(nm, np.broadcast_to(np.asarray(inp[key][:NSSD], np.float32)[None], (128, NSSD, 64)))
    put("ffn_conv_w", fm(inp["ffn_conv_w"][:NL], 88))
    put("ffn_conv_b", fm(inp["ffn_conv_b"][:NL], 88))
    return prm


_CACHE = {}


def run_cfg(cfg, inp, n_cores=8, trace=False):
    key = (cfg.NPT, cfg.TT, cfg.QP, cfg.NSEG, cfg.SL, cfg.NL)
    if key not in _CACHE:
        _CACHE[key] = Prog(cfg).build()
    nc = _CACHE[key]
    NL, NCONV, NSSD, NSEG = cfg.NL, cfg.NCONV, cfg.NSSD, cfg.NSEG
    xp = np.asarray(inp["x_prompt"], np.float32)
    xs = np.asarray(inp["x_sample"], np.float32)
    B = xp.shape[0]
    meta = np.asarray(inp["meta_tokens"], np.float32)
    prm = pack_params(cfg, inp)
    cst = make_consts()
    f32 = lambda a: np.ascontiguousarray(a, dtype=np.float32)
    wts = {
        "sc_w_in": f32(inp["sc_w_in"][:max(NCONV, 1)]),
        "sc_w_out": f32(inp["sc_w_out"][:max(NCONV, 1)]),
        "ssd_w_in": f32(inp["ssd_w_in"][:max(NSSD, 1)]),
        "ssd_w_out": f32(inp["ssd_w_out"][:max(NSSD, 1)]),
        "ffn_w_up": f32(inp["ffn_w_up"][:NL]),
        "ffn_w_down": f32(inp["ffn_w_down"][:NL]),
    }
    in_maps = []
    for c in range(n_cores):
        b = c % B
        sl = slice(c * NSEG, (c + 1) * NSEG)
        xin = np.concatenate([meta, xp[b], xs[sl].reshape(NSEG * cfg.SL, D)], axis=0)
        assert xin.shape[0] == cfg.NTOK
        m = dict(wts)
        m.update({
            "xin": f32(xin), "cst": cst, "prm": prm,
            "st_ca": f32(inp["state_conv_a"][:max(NCONV, 1), sl]),
            "st_sc": f32(inp["state_ssd_conv"][:max(NSSD, 1), sl]),
            "st_ss": f32(inp["state_ssd"][:max(NSSD, 1), sl]),
            "st_ff": f32(inp["state_ffn_conv"][:NL, sl]),
        })
        in_maps.append(m)
    res = run_bass_kernel_spmd(nc, in_maps, core_ids=list(range(n_cores)), trace=trace)
    R = res.results
    nm = meta.shape[0]
    y_prompt = np.stack([R[b]["y"][nm:cfg.NPT] for b in range(B)])
    y_sample = np.concatenate([R[c]["y"][cfg.NPT:].reshape(NSEG, cfg.SL, D) for c in range(n_cores)])

    def pstate(k):
        return np.stack([R[b][k][:, 0] for b in range(B)], axis=1)

    def sstate(k):
        return np.concatenate([R[c][k][:, 1:] for c in range(n_cores)], axis=1)
    outs = (y_prompt, y_sample, pstate("o_ca")[:NCONV], pstate("o_sc")[:NSSD], pstate("o_ss")[:NSSD], pstate("o_ff"),
            sstate("o_ca")[:NCONV], sstate("o_sc")[:NSSD], sstate("o_ss")[:NSSD], sstate("o_ff"))
    return tuple(np.ascontiguousarray(o, dtype=np.float32) for o in outs), res


def kernel(**inputs):
    cfg = Cfg()
    outs, _ = run_cfg(cfg, inputs)
    return outs
```

```python
import contextlib
import numpy as np
import concourse.bass as bass
import concourse.mybir as mybir
from concourse.bass_utils import run_bass_kernel_spmd

F32 = mybir.dt.float32
BF16 = mybir.dt.bfloat16
ALU = mybir.AluOpType
AF = mybir.ActivationFunctionType

D = 2048
DC = 16
DFF = 5632
FC = 44
HEADS = 64
GROUPS = 8
NSTATE = 128
INNER = 4096
CONVD = 6144
EPS = 1e-6
NEG = -1.0e30


class Op:
    __slots__ = ("eng", "fn", "deps", "signal", "dma", "slot", "seq", "cnt")

    def __init__(self, eng, fn, dma):
        self.eng = eng
        self.fn = fn
        self.dma = dma
        self.deps = []
        self.signal = False
        self.slot = -1
        self.seq = 0
        self.cnt = 0


class Sched:
    ENGS = ("pe", "act", "dve", "pool", "sp")

    def __init__(self, nslots=24):
        self.ops = {e: [] for e in self.ENGS}
        self.lastw = {}
        self.readers = {}
        self.fence = []
        self.nslots = nslots
        self.slot_ops = [[] for _ in range(nslots)]
        self.rr = 0
        self.rr_pool = 0
        self.nops = 0

    def add(self, eng, fn, reads=(), writes=(), dma=False, nobar=False):
        op = Op(eng, fn, dma)
        deps = {}

        def need(o):
            if o is None or o is op:
                return
            if (not o.dma) and (not dma) and o.eng == "pe" and eng == "pe":
                return
            deps[id(o)] = o

        for r in reads:
            need(self.lastw.get(r))
            if isinstance(r, tuple) and r[0] == "ps":
                rd = self.readers.get(r)
                if rd:
                    for o in rd.values():
                        if o.eng != eng:
                            need(o)
        for w in writes:
            need(self.lastw.get(w))
            rd = self.readers.get(w)
            if rd:
                for o in rd.values():
                    need(o)
        if not nobar:
            for o in self.fence:
                need(o)
        if dma:
            half = self.nslots // 2
            if eng == "pool":
                s = half + (self.rr_pool % half)
                self.rr_pool += 1
            else:
                s = self.rr % half
                self.rr += 1
            if self.slot_ops[s]:
                need(self.slot_ops[s][-1])
            self.slot_ops[s].append(op)
            op.slot = s
            op.seq = len(self.slot_ops[s])
        op.deps = list(deps.values())
        for o in op.deps:
            o.signal = True
        for r in reads:
            d = self.readers.setdefault(r, {})
            d[id(op) if dma else eng] = op
        for w in writes:
            self.lastw[w] = op
            self.readers[w] = {}
        self.ops[eng].append(op)
        self.nops += 1
        return op

    def barrier(self):
        self.fence = [self.ops[e][-1] for e in ("pe", "act", "dve", "sp") if self.ops[e]]

    def emit(self, nc, stack):
        sems = {e: stack.enter_context(nc.semaphore("sem_" + e)) for e in self.ENGS}
        slot_sems = [stack.enter_context(nc.semaphore("dsl%d" % i)) for i in range(self.nslots)]
        for e in self.ENGS:
            if self.ops[e]:
                last = self.ops[e][-1]
                if not last.dma:
                    last.signal = True
            c = 0
            for op in self.ops[e]:
                if (not op.dma) and op.signal:
                    c += 1
                    op.cnt = c
        final = {}
        for e in self.ENGS:
            n = 0
            for op in self.ops[e]:
                if not op.dma and op.signal:
                    n = op.cnt
            final[e] = n

        def ticket(o):
            if o.dma:
                return slot_sems[o.slot], 16 * o.seq
            return sems[o.eng], o.cnt

        def run(ename, eng):
            known = {}
            for op in self.ops[ename]:
                for o in op.deps:
                    s, v = ticket(o)
                    k = id(s)
                    if known.get(k, 0) < v:
                        eng.wait_ge(s, v)
                        known[k] = v
                ins = op.fn(eng)
                if op.dma:
                    ins.then_inc(slot_sems[op.slot], 16)
                elif op.signal:
                    ins.then_inc(sems[ename], 1)
            if ename == "sp":
                for i, so in enumerate(self.slot_ops):
                    if so:
                        eng.wait_ge(slot_sems[i], 16 * len(so))
                for e2 in ("pe", "act", "dve", "pool"):
                    if final[e2] > 0:
                        eng.wait_ge(sems[e2], final[e2])

        with nc.Block() as block:
            @block.tensor
            def _(e):
                run("pe", e)

            @block.scalar
            def _(e):
                run("act", e)

            @block.vector
            def _(e):
                run("dve", e)

            @block.gpsimd
            def _(e):
                run("pool", e)

            @block.sync
            def _(e):
                run("sp", e)


class StopBuild(Exception):
    pass


import os as _os
_STOP = _os.environ.get("SSD_STOP", "")


class Cfg:
    def __init__(self, npt=8208, tt=456, qp=114, nseg=4, sl=64, nl=4):
        self.NPT = npt
        self.TT = tt
        self.QP = qp
        self.NSEG = nseg
        self.SL = sl
        self.NL = nl
        assert npt % tt == 0 and tt % qp == 0 and tt <= 512 and qp <= 128
        assert nseg * sl <= 512
        self.NTOK = npt + nseg * sl
        self.NCONV = (nl + 1) // 2
        self.NSSD = nl // 2


def param_layout(cfg):
    off = {}
    o = 0

    def put(name, n):
        nonlocal o
        off[name] = (o, n)
        o += n
    put("norm_mix", cfg.NL * DC)
    put("norm_ffn", cfg.NL * DC)
    put("norm_final", DC)
    put("sc_conv_w", cfg.NCONV * 3 * DC)
    put("ssd_conv_w", max(cfg.NSSD, 1) * 4 * 48)
    put("ssd_conv_b", max(cfg.NSSD, 1) * 48)
    put("ffn_conv_w", cfg.NL * 3 * 88)
    put("ffn_conv_b", cfg.NL * 88)
    put("ssd_norm_w", max(cfg.NSSD, 1) * 32)
    put("dt_bias", max(cfg.NSSD, 1) * 64)
    put("a_log", max(cfg.NSSD, 1) * 64)
    put("ssd_d", max(cfg.NSSD, 1) * 64)
    return off, o


class Prog:
    def __init__(self, cfg):
        self.cfg = cfg
        self.S = Sched()
        self.nc = bass.Bass("TRN2", target_bir_lowering=False)
        self.stack = contextlib.ExitStack()
        self.bank_rr = 0
        self.uid = 0

    def sb(self, name, shape, dt):
        return self.stack.enter_context(self.nc.sbuf_tensor("sb_" + name, shape, dt))

    def dram_in(self, name, shape, dt=F32):
        return self.nc.dram_tensor(name, list(shape), dt, kind="ExternalInput").ap()

    def dram_out(self, name, shape, dt=F32):
        return self.nc.dram_tensor(name, list(shape), dt, kind="ExternalOutput").ap()

    def bank(self):
        b = self.bank_rr % 8
        self.bank_rr += 1
        return b

    def arena_reset(self):
        self.S.barrier()
        self.a_off = 0

    def carve(self, words, dt=F32, shape=None):
        a = self.a_off
        self.a_off += words
        assert self.a_off <= self.AW, ("arena overflow", self.a_off, self.AW)
        v = self.arena[:, a:a + words]
        if dt == BF16:
            v = v.bitcast(BF16)
        return v

    def mm(self, out, lhsT, rhs, start, stop, reads, writes):
        self.S.add("pe", lambda e: e.matmul(out, lhsT, rhs, start=start, stop=stop),
                   reads=reads, writes=writes)

    def tr(self, out, in_, ident, reads, writes):
        self.S.add("pe", lambda e: e.transpose(out, in_, ident), reads=reads, writes=writes)

    def act(self, out, in_, func, reads, writes, bias=None, scale=None, accum_out=None):
        kw = {}
        if bias is not None:
            kw["bias"] = bias
        if scale is not None:
            kw["scale"] = scale
        if accum_out is not None:
            kw["accum_out"] = accum_out
        self.S.add("act", lambda e: e.activation(out, in_, func, **kw), reads=reads, writes=writes)

    def tt(self, out, in0, in1, op, reads, writes, eng="dve"):
        self.S.add(eng, lambda e: e.tensor_tensor(out, in0, in1, op), reads=reads, writes=writes)

    def ts(self, out, in0, s1, s2, op0, op1, reads, writes, eng="dve"):
        if op1 is None:
            self.S.add(eng, lambda e: e.tensor_scalar(out, in0, s1, None, op0), reads=reads, writes=writes)
        else:
            self.S.add(eng, lambda e: e.tensor_scalar(out, in0, s1, s2, op0, op1), reads=reads, writes=writes)

    def stt(self, out, in0, scalar, in1, op0, op1, reads, writes, eng="dve"):
        self.S.add(eng, lambda e: e.scalar_tensor_tensor(out, in0, scalar, in1, op0, op1),
                   reads=reads, writes=writes)

    def cp(self, out, in_, reads, writes, eng="dve"):
        if eng == "act":
            self.S.add("act", lambda e: e.copy(out, in_), reads=reads, writes=writes)
        else:
            self.S.add(eng, lambda e: e.tensor_copy(out, in_), reads=reads, writes=writes)

    def memset(self, ap, val, writes, eng="dve"):
        self.S.add(eng, lambda e: e.memset(ap, val), writes=writes)

    def dma(self, out, in_, reads, writes, q="sp", nobar=False):
        self.S.add(q, lambda e: e.dma_start(out=out, in_=in_), reads=reads, writes=writes,
                   dma=True, nobar=nobar)

    def slab_plan_layer(self, l):
        cfg = self.cfg
        out = []
        j = l // 2

        def src(w, r0, kc, c0, wd):
            return w[r0:r0 + kc * 128, c0:c0 + wd].rearrange("(kc p) m -> p kc m", p=128)
        if l % 2 == 0:
            w_in = self.w["sc_w_in"][j]
            for i in range(DC):
                out.append(("sc_in", 16, 384, [(0, src(w_in, 0, 16, i * 128, 128)),
                                               (128, src(w_in, 0, 16, D + i * 128, 128)),
                                               (256, src(w_in, 0, 16, 2 * D + i * 128, 128))]))
            w_out = self.w["sc_w_out"][j]
            for m in range(4):
                out.append(("sc_out", 16, 512, [(0, src(w_out, 0, 16, m * 512, 512))]))
        else:
            w_in = self.w["ssd_w_in"][j]
            out.append(("ssd_dt", 16, 512, [(0, src(w_in, 0, 16, INNER + CONVD + 64 - 512, 512))]))
            for s in range(4):
                out.append(("ssd_bc", 16, 512, [(0, src(w_in, 0, 16, INNER + INNER + s * 512, 512))]))
            w_out = self.w["ssd_w_out"][j]
            for g in range(GROUPS):
                out.append(("ssd_z", 16, 512, [(0, src(w_in, 0, 16, g * 512, 512))]))
                out.append(("ssd_x", 16, 512, [(0, src(w_in, 0, 16, INNER + g * 512, 512))]))
                out.append(("ssd_out", 4, 2048, [(0, src(w_out, g * 512, 4, 0, 2048))]))
        w_up = self.w["ffn_w_up"][l]
        for s in range(FC // 2):
            out.append(("ffn_up", 16, 512, [(0, src(w_up, 0, 16, s * 256, 256)),
                                            (256, src(w_up, 0, 16, DFF + s * 256, 256))]))
        w_dn = self.w["ffn_w_down"][l]
        for m in range(4):
            for (k0, kc) in ((0, 16), (16, 16), (32, 12)):
                out.append(("ffn_dn", kc, 512, [(0, src(w_dn, k0 * 128, kc, m * 512, 512))]))
        return out

    def slab_issue(self, idx):
        tag, kc, wd, parts = self.slab_seq[idx]
        b = idx % self.NSLAB
        view = self.slab[b][:, 0:kc * wd].rearrange("p (k m) -> p k m", m=wd)
        for (co, s) in parts:
            ncol = s.shape[2]
            self.dma(view[:, :, co:co + ncol], s, reads=(), writes=(("slab", b),), q="pool", nobar=True)

    def next_slab(self, tag):
        idx = self.slab_pos
        self.slab_pos += 1
        t, kc, wd, parts = self.slab_seq[idx]
        assert t == tag, (t, tag, idx)
        while self.slab_issued < min(len(self.slab_seq), idx + self.NSLAB):
            self.slab_issue(self.slab_issued)
            self.slab_issued += 1
        b = idx % self.NSLAB
        view = self.slab[b][:, 0:kc * wd].rearrange("p (k m) -> p k m", m=wd)
        return view, ("slab", b)

    def build(self):
        cfg = self.cfg
        nc = self.nc
        NL, NCONV, NSSD, NSEG = cfg.NL, cfg.NCONV, cfg.NSSD, cfg.NSEG
        self.xin = self.dram_in("xin", [cfg.NTOK, D])
        self.cst_d = self.dram_in("cst", [128, 512])
        poff, pn = param_layout(cfg)
        self.poff = poff
        self.prm_d = self.dram_in("prm", [128, pn])
        self.w = {
            "sc_w_in": self.dram_in("sc_w_in", [max(NCONV, 1), D, 3 * D]),
            "sc_w_out": self.dram_in("sc_w_out", [max(NCONV, 1), D, D]),
            "ssd_w_in": self.dram_in("ssd_w_in", [max(NSSD, 1), D, 2 * INNER + 2 * GROUPS * NSTATE + HEADS]),
            "ssd_w_out": self.dram_in("ssd_w_out", [max(NSSD, 1), INNER, D]),
            "ffn_w_up": self.dram_in("ffn_w_up", [NL, D, 2 * DFF]),
            "ffn_w_down": self.dram_in("ffn_w_down", [NL, DFF, D]),
        }
        self.st_ca = self.dram_in("st_ca", [max(NCONV, 1), NSEG, 2, D])
        self.st_sc = self.dram_in("st_sc", [max(NSSD, 1), NSEG, 3, CONVD])
        self.st_ss = self.dram_in("st_ss", [max(NSSD, 1), NSEG, HEADS, 64, NSTATE])
        self.st_ff = self.dram_in("st_ff", [NL, NSEG, 2, 2 * DFF])
        self.y_d = self.dram_out("y", [cfg.NTOK, D])
        self.o_ca = self.dram_out("o_ca", [max(NCONV, 1), 1 + NSEG, 2, D])
        self.o_sc = self.dram_out("o_sc", [max(NSSD, 1), 1 + NSEG, 3, CONVD])
        self.o_ss = self.dram_out("o_ss", [max(NSSD, 1), 1 + NSEG, HEADS, 64, NSTATE])
        self.o_ff = self.dram_out("o_ff", [NL, 1 + NSEG, 2, 2 * DFF])
        self.sstate = nc.dram_tensor("sstate", [max(NSSD, 1), GROUPS, 128, 512], F32, kind="Internal").ap()

        XW = max(cfg.TT if cfg.NPT else 0, NSEG * cfg.SL)
        self.x = self.sb("x", [128, DC, XW], F32)
        self.h = self.sb("h", [128, DC, XW], BF16)
        self.NSLAB = 3
        self.slab = [self.sb("slab%d" % i, [128, 8192], BF16) for i in range(self.NSLAB)]
        self.cst = self.sb("cst", [128, 512], F32)
        self.prm = self.sb("prm", [128, pn], F32)
        self.ident_b = self.sb("ident_b", [128, 128], BF16)
        self.ones_b = self.sb("ones_b", [128, 128], BF16)
        self.tri_b = self.sb("tri_b", [128, 128], BF16)
        self.nega = self.sb("nega", [128, max(NSSD, 1) * 64], F32)
        self.sq = [self.sb("sq%d" % i, [128, 512], BF16) for i in range(2)]
        self.rstd = self.sb("rstd", [128, 512], F32)
        self.hp_ff = self.sb("hp_ff", [128, NL, 2 * 88], F32)
        self.hp_sc = self.sb("hp_sc", [128, max(NSSD, 1), 3 * 48], F32)
        self.hp_ca = self.sb("hp_ca", [128, max(NCONV, 1), 2 * DC], F32)
        self.hs_ff = self.sb("hs_ff", [128, NSEG * 2 * 88], F32)
        self.hs_sc = self.sb("hs_sc", [128, NSEG * 3 * 48], F32)
        self.hs_ca = self.sb("hs_ca", [128, NSEG * 2 * DC], F32)
        self.hstg = [self.sb("hstg%d" % i, [128, 128], F32) for i in range(2)]
        self.hstg_rr = 0
        self.cb2 = self.sb("cb2", [128, 2], F32)
        self.epsb = self.cb2[:, 0:1]
        self.oneb = self.cb2[:, 1:2]
        self.AW = 22000
        self.arena = self.sb("arena", [128, self.AW], F32)
        self.io = self.arena[:, self.AW - 2048:self.AW]
        self.ps = [self.stack.enter_context(nc.psum_tensor("ps%d" % i, [128, 512], F32)) for i in range(8)]

        self.ident_f = self.cst[:, 0:128]
        self.tri = self.cst[:, 128:256]
        self.maskneg = self.cst[:, 256:384]
        self.ones_f = self.cst[:, 384:512]

        tiles = []
        for k in range(cfg.NPT // cfg.TT):
            tiles.append(dict(kind="p", r0=k * cfg.TT, T=cfg.TT, nseg=1, L=cfg.TT, Q=cfg.QP,
                              first=(k == 0), last=(k == cfg.NPT // cfg.TT - 1)))
        if NSEG > 0:
            tiles.append(dict(kind="s", r0=cfg.NPT, T=NSEG * cfg.SL, nseg=NSEG, L=cfg.SL, Q=cfg.SL,
                              first=True, last=True))
        self.slab_seq = []
        for t in tiles:
            for l in range(NL):
                self.slab_seq += self.slab_plan_layer(l)
        self.slab_pos = 0
        self.slab_issued = 0

        self.a_off = 0
        self.dma(self.cst[:, :], self.cst_d, reads=(), writes=("cst",))
        self.dma(self.prm[:, :], self.prm_d, reads=(), writes=("prm",))
        self.cp(self.ident_b[:, :], self.ident_f, reads=("cst",), writes=("identb",))
        self.cp(self.ones_b[:, :], self.ones_f, reads=("cst",), writes=("onesb",))
        self.cp(self.tri_b[:, :], self.tri, reads=("cst",), writes=("trib",))
        if NSSD > 0:
            o, n = poff["a_log"]
            self.act(self.nega[:, :], self.prm[:, o:o + n], AF.Exp, reads=("prm",), writes=("nega",))
            self.ts(self.nega[:, :], self.nega[:, :], -1.0, None, ALU.mult, None, reads=("nega",), writes=("nega",))
        self.memset(self.hp_ff[:, :, :], 0.0, writes=("hff",))
        self.memset(self.hp_sc[:, :, :], 0.0, writes=("hsc",))
        self.memset(self.hp_ca[:, :, :], 0.0, writes=("hca",))
        self.memset(self.cb2[:, 0:1], EPS, writes=("cb2",))
        self.memset(self.cb2[:, 1:2], 1.0, writes=("cb2",))
        self.S.barrier()

        try:
            for t in tiles:
                self.tile = t
                self.run_tile(t)
        except StopBuild:
            pass

        self.S.emit(nc, self.stack)
        self.stack.close()
        return nc

    def P(self, name, idx, n):
        o, tot = self.poff[name]
        return self.prm[:, o + idx * n:o + (idx + 1) * n]

    def run_tile(self, t):
        cfg = self.cfg
        self.load_tile(t)
        for l in range(cfg.NL):
            self.rmsnorm(t, ("norm_mix", l))
            if l % 2 == 0:
                self.conv_mixer(t, l)
            else:
                self.ssd_mixer(t, l)
            self.rmsnorm(t, ("norm_ffn", l))
            self.ffn(t, l)
        self.final_out(t)

    def tok_groups(self, T):
        out = []
        o = 0
        while o < T:
            n = min(128, T - o)
            out.append((o, n))
            o += n
        return out

    def load_tile(self, t):
        T = t["T"]
        for (o, n) in self.tok_groups(T):
            self.dma(self.io[0:n, :], self.xin[t["r0"] + o:t["r0"] + o + n, :], reads=(), writes=("io",))
            for cq in range(4):
                b = self.bank()
                for j in range(4):
                    c = cq * 4 + j
                    self.tr(self.ps[b][:, j * 128:j * 128 + n], self.io[0:n, c * 128:(c + 1) * 128],
                            self.ident_f[0:n, 0:n], reads=("io", "cst"), writes=(("ps", b),))
                src = self.ps[b][:, :].rearrange("p (j m) -> p j m", m=128)[:, :, 0:n]
                dst = self.x[:, cq * 4:cq * 4 + 4, o:o + n]
                self.cp(dst, src, reads=(("ps", b),), writes=tuple(("x", cq * 4 + j) for j in range(4)),
                        eng=("act" if cq % 2 == 0 else "dve"))

    def rms_stats(self, t):
        T = t["T"]
        b = self.bank()
        for c in range(DC):
            s = self.sq[c % 2]
            self.act(s[:, 0:T], self.x[:, c, 0:T], AF.Square, reads=(("x", c),), writes=(("sq", c % 2),))
            self.mm(self.ps[b][:, 0:T], self.ones_b[:, :], s[:, 0:T], start=(c == 0), stop=(c == DC - 1),
                    reads=(("sq", c % 2), "onesb"), writes=(("ps", b),))
        self.act(self.rstd[:, 0:T], self.ps[b][:, 0:T], AF.Sqrt, reads=(("ps", b),), writes=("rstd",),
                 bias=self.epsb[:, 0:1], scale=1.0 / D)
        self.S.add("dve", lambda e: e.reciprocal(self.rstd[:, 0:T], self.rstd[:, 0:T]),
                   reads=("rstd",), writes=("rstd",))

    def rmsnorm(self, t, which):
        T = t["T"]
        name, l = which
        self.rms_stats(t)
        w = self.P(name, l, DC)
        for c in range(DC):
            self.stt(self.h[:, c, 0:T], self.x[:, c, 0:T], w[:, c:c + 1], self.rstd[:, 0:T], ALU.mult, ALU.mult,
                     reads=(("x", c), "rstd", "prm"), writes=(("h", c),))

    def final_out(self, t):
        T = t["T"]
        self.arena_reset()
        yq = [self.carve(512).rearrange("p (j m) -> p j m", m=128) for _ in range(2)]
        self.rms_stats(t)
        w = self.P("norm_final", 0, DC)
        qi = 0
        for (o, n) in self.tok_groups(T):
            for cq in range(4):
                q = yq[qi % 2]
                rq = ("yq", qi % 2)
                qi += 1
                for j in range(4):
                    c = cq * 4 + j
                    self.stt(q[:, j, 0:n], self.x[:, c, o:o + n], w[:, c:c + 1], self.rstd[:, o:o + n],
                             ALU.mult, ALU.mult, reads=(("x", c), "rstd", "prm"), writes=(rq,))
                b = self.bank()
                for j in range(4):
                    self.tr(self.ps[b][0:n, j * 128:(j + 1) * 128], q[:, j, 0:n], self.ident_f,
                            reads=(rq, "cst"), writes=(("ps", b),))
                self.cp(self.io[0:n, cq * 512:(cq + 1) * 512], self.ps[b][0:n, :], reads=(("ps", b),),
                        writes=("io",), eng="act")
            self.dma(self.y_d[t["r0"] + o:t["r0"] + o + n, :], self.io[0:n, :], reads=("io",), writes=())

    def halo_view(self, t, kind, l, taps, nch):
        hw = taps - 1
        if t["kind"] == "p":
            buf = {"ff": self.hp_ff, "sc": self.hp_sc, "ca": self.hp_ca}[kind]
            li = l if kind == "ff" else l // 2
            return buf[:, li:li + 1, :].rearrange("p s (t c) -> p s t c", c=nch), (("h" + kind),)
        buf = {"ff": self.hs_ff, "sc": self.hs_sc, "ca": self.hs_ca}[kind]
        return buf[:, :].rearrange("p (s t c) -> p s t c", t=hw, c=nch), (("hs" + kind),)

    def halo_load_sample(self, t, kind, l, taps, nch):
        if t["kind"] != "s":
            return
        hw = taps - 1
        nseg = t["nseg"]
        src = {"ff": self.st_ff, "sc": self.st_sc, "ca": self.st_ca}[kind]
        li = l if kind == "ff" else l // 2
        rows = src[li].rearrange("s t (c p) -> (s t c) p", p=128)
        buf = {"ff": self.hs_ff, "sc": self.hs_sc, "ca": self.hs_ca}[kind]
        res = ("hs" + kind)
        nrows = nseg * hw * nch
        r = 0
        while r < nrows:
            n = min(128, nrows - r)
            st = self.hstg[self.hstg_rr % 2]
            rs = ("hstg", self.hstg_rr % 2)
            self.hstg_rr += 1
            self.dma(st[0:n, :], rows[r:r + n, :], reads=(), writes=(rs,))
            b = self.bank()
            self.tr(self.ps[b][:, 0:n], st[0:n, :], self.ident_f[0:n, 0:n], reads=(rs, "cst"), writes=(("ps", b),))
            self.cp(buf[:, r:r + n], self.ps[b][:, 0:n], reads=(("ps", b),),
                    writes=tuple((res, ci) for ci in range(nch)), eng="act")
            r += n

    def halo_store(self, t, kind, l, taps, nch):
        if not t["last"]:
            return
        hw = taps - 1
        li = l if kind == "ff" else l // 2
        dst_t = {"ff": self.o_ff, "sc": self.o_sc, "ca": self.o_ca}[kind]
        if t["kind"] == "p":
            buf = {"ff": self.hp_ff, "sc": self.hp_sc, "ca": self.hp_ca}[kind][:, li, :]
            res = ("h" + kind)
            rows = dst_t[li, 0:1].rearrange("s t (c p) -> (s t c) p", p=128)
            nrows = hw * nch
        else:
            buf = {"ff": self.hs_ff, "sc": self.hs_sc, "ca": self.hs_ca}[kind][:, :]
            res = ("hs" + kind)
            rows = dst_t[li, 1:1 + t["nseg"]].rearrange("s t (c p) -> (s t c) p", p=128)
            nrows = t["nseg"] * hw * nch
        r = 0
        while r < nrows:
            n = min(128, nrows - r)
            st = self.hstg[self.hstg_rr % 2]
            rs = ("hstg", self.hstg_rr % 2)
            self.hstg_rr += 1
            b = self.bank()
            self.tr(self.ps[b][0:n, 0:128], buf[:, r:r + n], self.ident_f,
                    reads=tuple((res, ci) for ci in range(nch)) + ("cst",), writes=(("ps", b),))
            self.cp(st[0:n, :], self.ps[b][0:n, 0:128], reads=(("ps", b),), writes=(rs,), eng="act")
            self.dma(rows[r:r + n, :], st[0:n, :], reads=(rs,), writes=())
            r += n

    def conv_chunk(self, t, src_ap, src_res, inb, inb_res, acc, acc_res, halo, halo_res, ci, taps,
                   wts, bias, evac_eng="act", src_is_sbuf_tt=None):
        nseg, L, T = t["nseg"], t["L"], t["T"]
        hw = taps - 1
        Pp = hw + L
        TP = nseg * Pp
        inv = inb[:, 0:TP].rearrange("p (s q) -> p s q", q=Pp)
        accv = acc[:, 0:TP].rearrange("p (s q) -> p s q", q=Pp)
        srcv = src_ap.rearrange("p (s q) -> p s q", q=L)
        if src_is_sbuf_tt is None:
            self.cp(inv[:, :, hw:], srcv, reads=(src_res,), writes=(inb_res,), eng=evac_eng)
        else:
            other = src_is_sbuf_tt[0].rearrange("p (s q) -> p s q", q=L)
            self.tt(inv[:, :, hw:], other, srcv, ALU.mult, reads=(src_res, src_is_sbuf_tt[1]), writes=(inb_res,))
        self.cp(inv[:, :, 0:hw], halo[:, :, :, ci], reads=(halo_res + (ci,),), writes=(inb_res,), eng="act")
        if bias is not None:
            self.ts(acc[:, hw:TP], inb[:, hw:TP], wts[taps - 1], bias, ALU.mult, ALU.add,
                    reads=(inb_res, "prm"), writes=(acc_res,))
        else:
            self.ts(acc[:, hw:TP], inb[:, hw:TP], wts[taps - 1], None, ALU.mult, None,
                    reads=(inb_res, "prm"), writes=(acc_res,))
        for k in range(1, taps):
            self.stt(acc[:, hw:TP], inb[:, hw - k:TP - k], wts[taps - 1 - k], acc[:, hw:TP], ALU.mult, ALU.add,
                     reads=(inb_res, acc_res, "prm"), writes=(acc_res,))
        self.cp(halo[:, :, :, ci], inv[:, :, L:L + hw], reads=(inb_res,), writes=(halo_res + (ci,),), eng="act")
        return accv[:, :, hw:]

    def conv_mixer(self, t, l):
        T, nseg, L = t["T"], t["nseg"], t["L"]
        j = l // 2
        self.arena_reset()
        y = self.carve(DC * 256, BF16).rearrange("p (c m) -> p c m", m=512)
        tmpc = [self.carve(512) for _ in range(2)]
        inb = [self.carve(544) for _ in range(2)]
        acc = [self.carve(544) for _ in range(2)]
        self.halo_load_sample(t, "ca", l, 3, DC)
        halo, hres = self.halo_view(t, "ca", l, 3, DC)
        cw = self.P("sc_conv_w", j, 3 * DC)
        for i in range(DC):
            sl, sres = self.next_slab("sc_in")
            bb, bc, bv = self.bank(), self.bank(), self.bank()
            for (bk, co) in ((bc, 128), (bv, 256), (bb, 0)):
                for kc in range(DC):
                    self.mm(self.ps[bk][:, 0:T], sl[:, kc, co:co + 128], self.h[:, kc, 0:T],
                            start=(kc == 0), stop=(kc == DC - 1), reads=(sres, ("h", kc)), writes=(("ps", bk),))
            k = i % 2
            self.cp(tmpc[k][:, 0:T], self.ps[bc][:, 0:T], reads=(("ps", bc),), writes=(("tmpc", k),), eng="act")
            wts = [cw[:, tp * DC + i:tp * DC + i + 1] for tp in range(3)]
            res = self.conv_chunk(t, self.ps[bv][:, 0:T], ("ps", bv), inb[k], ("inb", k), acc[k], ("acc", k),
                                  halo, hres, i, 3, wts, None, src_is_sbuf_tt=(tmpc[k][:, 0:T], ("tmpc", k)))
            yv = y[:, i, 0:T].rearrange("p (s q) -> p s q", q=L)
            bsrc = self.ps[bb][:, 0:T].rearrange("p (s q) -> p s q", q=L)
            self.tt(yv, res, bsrc, ALU.mult, reads=(("acc", k), ("ps", bb)), writes=(("y", i),))
        for m in range(4):
            sl, sres = self.next_slab("sc_out")
            for mc in range(4):
                b = self.bank()
                c = m * 4 + mc
                for kc in range(DC):
                    self.mm(self.ps[b][:, 0:T], sl[:, kc, mc * 128:(mc + 1) * 128], y[:, kc, 0:T],
                            start=(kc == 0), stop=(kc == DC - 1), reads=(sres, ("y", kc)), writes=(("ps", b),))
                self.tt(self.x[:, c, 0:T], self.x[:, c, 0:T], self.ps[b][:, 0:T], ALU.add,
                        reads=(("x", c), ("ps", b)), writes=(("x", c),))
        self.halo_store(t, "ca", l, 3, DC)

    def ffn(self, t, l):
        T, nseg, L = t["T"], t["nseg"], t["L"]
        self.arena_reset()
        g = self.carve(FC * 256, BF16).rearrange("p (c m) -> p c m", m=512)
        ina = [self.carve(544) for _ in range(2)]
        inv_ = [self.carve(544) for _ in range(2)]
        acca = [self.carve(544) for _ in range(2)]
        accv = [self.carve(544) for _ in range(2)]
        sa = [self.carve(544) for _ in range(2)]
        self.halo_load_sample(t, "ff", l, 3, 88)
        halo, hres = self.halo_view(t, "ff", l, 3, 88)
        cw = self.P("ffn_conv_w", l, 3 * 88)
        cb = self.P("ffn_conv_b", l, 88)
        hw = 2
        Pp = hw + L
        TP = nseg * Pp
        for s in range(FC // 2):
            sl, sres = self.next_slab("ffn_up")
            for q2 in range(2):
                q = s * 2 + q2
                k = q % 2
                ba, bv = self.bank(), self.bank()
                for (bk, co) in ((ba, q2 * 128), (bv, 256 + q2 * 128)):
                    for kc in range(DC):
                        self.mm(self.ps[bk][:, 0:T], sl[:, kc, co:co + 128], self.h[:, kc, 0:T],
                                start=(kc == 0), stop=(kc == DC - 1), reads=(sres, ("h", kc)), writes=(("ps", bk),))
                wa = [cw[:, tp * 88 + q:tp * 88 + q + 1] for tp in range(3)]
                wv = [cw[:, tp * 88 + FC + q:tp * 88 + FC + q + 1] for tp in range(3)]
                ra = self.conv_chunk(t, self.ps[ba][:, 0:T], ("ps", ba), ina[k], ("ina", k), acca[k], ("acca", k),
                                     halo, hres, q, 3, wa, cb[:, q:q + 1])
                rv = self.conv_chunk(t, self.ps[bv][:, 0:T], ("ps", bv), inv_[k], ("inv", k), accv[k], ("accv", k),
                                     halo, hres, FC + q, 3, wv, cb[:, FC + q:FC + q + 1])
                self.act(sa[k][:, hw:TP], acca[k][:, hw:TP], AF.Silu, reads=(("acca", k),), writes=(("sa", k),))
                sav = sa[k][:, 0:TP].rearrange("p (s q) -> p s q", q=Pp)[:, :, hw:]
                gv = g[:, q, 0:T].rearrange("p (s q) -> p s q", q=L)
                self.tt(gv, sav, rv, ALU.mult, reads=(("sa", k), ("accv", k)), writes=(("g", q),))
        for m in range(4):
            bs = [self.bank() for _ in range(4)]
            for (k0, kcn) in ((0, 16), (16, 16), (32, 12)):
                sl, sres = self.next_slab("ffn_dn")
                for mc in range(4):
                    for kc in range(kcn):
                        kk = k0 + kc
                        self.mm(self.ps[bs[mc]][:, 0:T], sl[:, kc, mc * 128:(mc + 1) * 128], g[:, kk, 0:T],
                                start=(kk == 0), stop=(kk == FC - 1), reads=(sres, ("g", kk)),
                                writes=(("ps", bs[mc]),))
            for mc in range(4):
                c = m * 4 + mc
                self.tt(self.x[:, c, 0:T], self.x[:, c, 0:T], self.ps[bs[mc]][:, 0:T], ALU.add,
                        reads=(("x", c), ("ps", bs[mc])), writes=(("x", c),))
        self.halo_store(t, "ff", l, 3, 88)

    def ssd_mixer(self, t, l):
        cfg = self.cfg
        T, nseg, L, Q = t["T"], t["nseg"], t["L"], t["Q"]
        j = l // 2
        nck = T // Q
        cps = L // Q
        self.arena_reset()
        zs = self.carve(4 * 256, BF16).rearrange("p (c m) -> p c m", m=512)
        xtm = self.carve(4 * 256, BF16).rearrange("p (c m) -> p c m", m=512)
        gfm = self.carve(4 * 256, BF16).rearrange("p (c m) -> p c m", m=512)
        bfm = self.carve(8 * 256, BF16).rearrange("p (c m) -> p c m", m=512)
        cfm = self.carve(8 * 256, BF16).rearrange("p (c m) -> p c m", m=512)
        btm = self.carve(4 * 512, BF16).rearrange("p (c m) -> p c m", m=1024)
        def dtbuf():
            return self.carve(4 * 64).rearrange("p (c m) -> p c m", m=64)
        dt_a, dta_a, cs_a, ecs_a, wt_a = dtbuf(), dtbuf(), dtbuf(), dtbuf(), dtbuf()
        el_a = dtbuf()
        d3 = self.carve(4 * 3 * 32, BF16).rearrange("p (c s m) -> p c s m", s=3, m=64)
        tmp64 = [self.carve(64) for _ in range(3)]
        inb = [self.carve(544) for _ in range(2)]
        acc = [self.carve(544) for _ in range(2)]
        xc = [self.carve(256, BF16) for _ in range(2)]
        cbg = [self.carve(128) for _ in range(2)]
        seg = [self.carve(1024).rearrange("p (h m) -> p h m", m=128) for _ in range(2)]
        Mb = [self.carve(512, BF16).rearrange("p (h m) -> p h m", m=128) for _ in range(2)]
        xw = [self.carve(256, BF16) for _ in range(2)]
        yb = [self.carve(512) for _ in range(2)]
        ytmp = self.carve(512)
        ss = [self.carve(2) for _ in range(2)]
        junk = self.carve(512)
        Sf = [self.carve(512) for _ in range(2)]
        Sb = [self.carve(256, BF16) for _ in range(2)]
        stg = self.carve(512).rearrange("p (b m) -> p b m", m=128)

        self.halo_load_sample(t, "sc", l, 4, 48)
        halo, hres = self.halo_view(t, "sc", l, 4, 48)
        cw = self.P("ssd_conv_w", j, 4 * 48)
        cb = self.P("ssd_conv_b", j, 48)
        dtb = self.P("dt_bias", j, 64)
        dsk = self.P("ssd_d", j, 64)
        nw = self.P("ssd_norm_w", j, 32)
        nega = self.nega[:, j * 64:(j + 1) * 64]

        def tok(c):
            return slice(c * Q, (c + 1) * Q)

        sl, sres = self.next_slab("ssd_dt")
        for c in range(nck):
            b = self.bank()
            for kc in range(DC):
                self.mm(self.ps[b][0:Q, 0:64], self.h[:, kc, tok(c)], sl[:, kc, 448:512], start=(kc == 0),
                        stop=(kc == DC - 1), reads=(sres, ("h", kc)), writes=(("ps", b),))
            xd, ax, ee = tmp64[0], tmp64[1], tmp64[2]
            self.tt(xd[0:Q, :], self.ps[b][0:Q, 0:64], dtb[0:Q, :], ALU.add, reads=(("ps", b), "prm"), writes=("xd",))
            self.act(ax[0:Q, :], xd[0:Q, :], AF.Abs, reads=("xd",), writes=("ax",))
            self.act(ee[0:Q, :], ax[0:Q, :], AF.Exp, reads=("ax",), writes=("ee",), scale=-1.0)
            self.act(ee[0:Q, :], ee[0:Q, :], AF.Ln, reads=("ee",), writes=("ee",), bias=self.oneb[0:Q, 0:1])
            self.stt(dt_a[0:Q, c, :], xd[0:Q, :], 0.0, ee[0:Q, :], ALU.max, ALU.add, reads=("xd", "ee"),
                     writes=(("dt", c),))
            self.tt(dta_a[0:Q, c, :], dt_a[0:Q, c, :], nega[0:Q, :], ALU.mult, reads=(("dt", c), "nega"),
                    writes=(("dta", c),))
            r1, r2 = tmp64[1], tmp64[2]
            self.cp(d3[0:Q, c, 0, :], dta_a[0:Q, c, :], reads=(("dta", c),), writes=(("d3", c),))
            self.tt(r1[0:Q, :], dta_a[0:Q, c, :], d3[0:Q, c, 0, :], ALU.subtract, reads=(("dta", c), ("d3", c)),
                    writes=("ax",))
            self.cp(d3[0:Q, c, 1, :], r1[0:Q, :], reads=("ax",), writes=(("d3", c),))
            self.tt(r2[0:Q, :], r1[0:Q, :], d3[0:Q, c, 1, :], ALU.subtract, reads=("ax", ("d3", c)), writes=("ee",))
            self.cp(d3[0:Q, c, 2, :], r2[0:Q, :], reads=("ee",), writes=(("d3", c),))
            b1, b2 = self.bank(), self.bank()
            for s3 in range(3):
                self.mm(self.ps[b1][0:Q, 0:64], self.tri_b[0:Q, 0:Q], d3[0:Q, c, s3, :], start=(s3 == 0), stop=(s3 == 2),
                        reads=("trib", ("d3", c)), writes=(("ps", b1),))
            for s3 in range(3):
                self.mm(self.ps[b2][:, 0:64], self.ones_b[0:Q, :], d3[0:Q, c, s3, :], start=(s3 == 0), stop=(s3 == 2),
                        reads=("onesb", ("d3", c)), writes=(("ps", b2),))
            self.cp(cs_a[0:Q, c, :], self.ps[b1][0:Q, 0:64], reads=(("ps", b1),), writes=(("cs", c),))
            self.act(ecs_a[0:Q, c, :], self.ps[b1][0:Q, 0:64], AF.Exp, reads=(("ps", b1),), writes=(("ecs", c),))
            self.act(el_a[:, c, :], self.ps[b2][:, 0:64], AF.Exp, reads=(("ps", b2),), writes=(("el", c),))
            self.tt(wt_a[0:Q, c, :], self.ps[b2][0:Q, 0:64], cs_a[0:Q, c, :], ALU.subtract,
                    reads=(("ps", b2), ("cs", c)), writes=(("wt", c),))
            self.act(wt_a[0:Q, c, :], wt_a[0:Q, c, :], AF.Exp, reads=(("wt", c),), writes=(("wt", c),))
            self.tt(wt_a[0:Q, c, :], wt_a[0:Q, c, :], dt_a[0:Q, c, :], ALU.mult, reads=(("wt", c), ("dt", c)),
                    writes=(("wt", c),))

        if _STOP == "dt":
            raise StopBuild()
        def xbc_chunk(sl, sres, col, ci, dst_ap, dst_res, k):
            b = self.bank()
            for kc in range(DC):
                self.mm(self.ps[b][:, 0:T], sl[:, kc, col:col + 128], self.h[:, kc, 0:T], start=(kc == 0),
                        stop=(kc == DC - 1), reads=(sres, ("h", kc)), writes=(("ps", b),))
            wts = [cw[:, tp * 48 + ci:tp * 48 + ci + 1] for tp in range(4)]
            res = self.conv_chunk(t, self.ps[b][:, 0:T], ("ps", b), inb[k], ("inb", k), acc[k], ("acc", k),
                                  halo, hres, ci, 4, wts, cb[:, ci:ci + 1])
            dv = dst_ap.rearrange("p (s q) -> p s q", q=L)
            self.act(dv, res, AF.Silu, reads=(("acc", k),), writes=(dst_res,))

        kk = 0
        for s4 in range(4):
            sl, sres = self.next_slab("ssd_bc")
            for mc in range(4):
                gi = (s4 % 2) * 4 + mc
                if s4 < 2:
                    xbc_chunk(sl, sres, mc * 128, 32 + gi, bfm[:, gi, 0:T], ("bfm", gi), kk % 2)
                    for c in range(nck):
                        b = self.bank()
                        pb = self.ps[b][:, :].bitcast(BF16)
                        self.tr(pb[0:Q, 0:128], bfm[:, gi, tok(c)], self.ident_b[:, :], reads=(("bfm", gi), "identb"),
                                writes=(("ps", b),))
                        self.cp(btm[0:Q, c, gi * 128:(gi + 1) * 128], pb[0:Q, 0:128], reads=(("ps", b),),
                                writes=(("btm", c, gi),), eng="act")
                else:
                    xbc_chunk(sl, sres, mc * 128, 40 + gi, cfm[:, gi, 0:T], ("cfm", gi), kk % 2)
                kk += 1

        if _STOP == "bc":
            raise StopBuild()
        for g in range(GROUPS):
            sl, sres = self.next_slab("ssd_z")
            for c in range(nck):
                b = self.bank()
                for kc in range(DC):
                    self.mm(self.ps[b][0:Q, 0:512], self.h[:, kc, tok(c)], sl[:, kc, 0:512], start=(kc == 0),
                            stop=(kc == DC - 1), reads=(sres, ("h", kc)), writes=(("ps", b),))
                self.act(zs[0:Q, c, :], self.ps[b][0:Q, 0:512], AF.Silu, reads=(("ps", b),), writes=(("zs", c),))
            sl, sres = self.next_slab("ssd_x")
            for cc in range(4):
                k = kk % 2
                kk += 1
                xbc_chunk(sl, sres, cc * 128, g * 4 + cc, xc[k][:, 0:T], ("xc", k), k)
                for c in range(nck):
                    b = self.bank()
                    pb = self.ps[b][:, :].bitcast(BF16)
                    self.tr(pb[0:Q, 0:128], xc[k][:, tok(c)], self.ident_b[:, :], reads=(("xc", k), "identb"),
                            writes=(("ps", b),))
                    self.cp(xtm[0:Q, c, cc * 128:(cc + 1) * 128], pb[0:Q, 0:128], reads=(("ps", b),),
                            writes=(("xtm", c),), eng="act")
            if _STOP == "zx":
                raise StopBuild()
            for c in range(nck):
                sg = c // cps
                first = (c % cps == 0)
                last = (c % cps == cps - 1)
                sk = (sg % 2) if t["kind"] == "s" else 0
                S_, Sb_ = Sf[sk], Sb[sk]
                rS, rSb = ("S", sk), ("Sb", sk)
                if first:
                    if t["kind"] == "p" and t["first"]:
                        self.memset(S_[:, :], 0.0, writes=(rS,))
                        self.memset(Sb_[:, :], 0.0, writes=(rSb,))
                    elif t["kind"] == "p":
                        self.dma(S_[:, :], self.sstate[j, g], reads=("sstate",), writes=(rS,))
                        self.cp(Sb_[:, :], S_[:, :], reads=(rS,), writes=(rSb,), eng="act")
                    else:
                        srcst = self.st_ss[j, sg, g * 8:(g + 1) * 8].rearrange("h p n -> (h p) n") \
                            .rearrange("(b r) n -> r b n", r=128)
                        self.dma(stg[:, :, :], srcst, reads=(), writes=("stg",))
                        b = self.bank()
                        for bl in range(4):
                            self.tr(self.ps[b][:, bl * 128:(bl + 1) * 128], stg[:, bl, :], self.ident_f,
                                    reads=("stg", "cst"), writes=(("ps", b),))
                        self.cp(S_[:, :], self.ps[b][:, :], reads=(("ps", b),), writes=(rS,))
                        self.cp(Sb_[:, :], self.ps[b][:, :], reads=(("ps", b),), writes=(rSb,), eng="act")
                k = (g * nck + c) % 2
                b = self.bank()
                self.mm(self.ps[b][0:Q, 0:Q], bfm[:, g, tok(c)], cfm[:, g, tok(c)], start=True, stop=True,
                        reads=(("bfm", g), ("cfm", g)), writes=(("ps", b),))
                self.cp(cbg[k][0:Q, 0:Q], self.ps[b][0:Q, 0:Q], reads=(("ps", b),), writes=(("cbg", k),), eng="act")
                for hh in range(2):
                    b = self.bank()
                    for h4 in range(4):
                        hl = hh * 4 + h4
                        hd = g * 8 + hl
                        for s3 in range(3):
                            lhs = d3[0:Q, c, s3, hd:hd + 1].broadcast_to([Q, Q])
                            self.mm(self.ps[b][0:Q, h4 * 128:h4 * 128 + Q], lhs, self.tri_b[0:Q, 0:Q], start=(s3 == 0),
                                    stop=(s3 == 2), reads=(("d3", c), "trib"), writes=(("ps", b),))
                    for h4 in range(4):
                        hl = hh * 4 + h4
                        hd = g * 8 + hl
                        self.stt(seg[k][0:Q, hl, 0:Q], self.ps[b][0:Q, h4 * 128:h4 * 128 + Q],
                                 cs_a[0:Q, c, hd:hd + 1], self.maskneg[0:Q, 0:Q], ALU.subtract, ALU.add,
                                 reads=(("ps", b), ("cs", c), "cst"), writes=(("seg", k),))
                self.act(seg[k][0:Q, :, 0:Q], seg[k][0:Q, :, 0:Q], AF.Exp, reads=(("seg", k),), writes=(("seg", k),))
                for hl in range(8):
                    hd = g * 8 + hl
                    self.stt(Mb[k][0:Q, hl, 0:Q], seg[k][0:Q, hl, 0:Q], dt_a[0:Q, c, hd:hd + 1], cbg[k][0:Q, 0:Q],
                             ALU.mult, ALU.mult, reads=(("seg", k), ("dt", c), ("cbg", k)), writes=(("M", k),))
                bi, ba = self.bank(), self.bank()
                self.mm(self.ps[bi][0:Q, 0:512], cfm[:, g, tok(c)], Sb_[:, :], start=True, stop=True,
                        reads=(("cfm", g), rSb), writes=(("ps", bi),))
                for hl in range(8):
                    self.mm(self.ps[ba][0:Q, hl * 64:(hl + 1) * 64], Mb[k][0:Q, hl, 0:Q], xtm[0:Q, c, hl * 64:(hl + 1) * 64],
                            start=True, stop=True, reads=(("M", k), ("xtm", c)), writes=(("ps", ba),))
                yv = yb[k][0:Q, :].rearrange("p (h m) -> p h m", m=64)
                ecsb = ecs_a[0:Q, c, g * 8:(g + 1) * 8].unsqueeze(2).broadcast_to([Q, 8, 64])
                self.tt(yv, self.ps[bi][0:Q, 0:512].rearrange("p (h m) -> p h m", m=64), ecsb, ALU.mult,
                        reads=(("ps", bi), ("ecs", c)), writes=(("yb", k),))
                self.tt(yb[k][0:Q, :], yb[k][0:Q, :], self.ps[ba][0:Q, 0:512], ALU.add,
                        reads=(("yb", k), ("ps", ba)), writes=(("yb", k),))
                dskb = dsk[0:Q, g * 8:(g + 1) * 8].unsqueeze(2).broadcast_to([Q, 8, 64])
                self.tt(ytmp[0:Q, :].rearrange("p (h m) -> p h m", m=64),
                        xtm[0:Q, c, :].rearrange("p (h m) -> p h m", m=64), dskb, ALU.mult,
                        reads=(("xtm", c), "prm"), writes=("ytmp",))
                self.tt(yb[k][0:Q, :], yb[k][0:Q, :], ytmp[0:Q, :], ALU.add, reads=(("yb", k), "ytmp"),
                        writes=(("yb", k),))
                self.tt(yb[k][0:Q, :], yb[k][0:Q, :], zs[0:Q, c, :], ALU.mult, reads=(("yb", k), ("zs", c)),
                        writes=(("yb", k),))
                self.act(junk[0:Q, :], yb[k][0:Q, :], AF.Square, reads=(("yb", k),), writes=("junk", ("ss", k)),
                         accum_out=ss[k][0:Q, 0:1])
                self.act(ss[k][0:Q, 0:1], ss[k][0:Q, 0:1], AF.Sqrt, reads=(("ss", k),), writes=(("ss", k),),
                         bias=self.epsb[0:Q, 0:1], scale=1.0 / 512.0)
                self.S.add("dve", lambda e, k=k: e.reciprocal(ss[k][0:Q, 0:1], ss[k][0:Q, 0:1]),
                           reads=(("ss", k),), writes=(("ss", k),))
                self.ts(zs[0:Q, c, :], yb[k][0:Q, :], ss[k][0:Q, 0:1], None, ALU.mult, None,
                        reads=(("yb", k), ("ss", k)), writes=(("zs", c),))
                wtb = wt_a[0:Q, c, g * 8:(g + 1) * 8].unsqueeze(2).broadcast_to([Q, 8, 64])
                self.tt(xw[k][0:Q, :].rearrange("p (h m) -> p h m", m=64),
                        xtm[0:Q, c, :].rearrange("p (h m) -> p h m", m=64), wtb, ALU.mult,
                        reads=(("xtm", c), ("wt", c)), writes=(("xw", k),))
                bu = self.bank()
                self.mm(self.ps[bu][:, 0:512], btm[0:Q, c, g * 128:(g + 1) * 128], xw[k][0:Q, :], start=True, stop=True,
                        reads=(("btm", c, g), ("xw", k)), writes=(("ps", bu),))
                elb = el_a[:, c, g * 8:(g + 1) * 8].unsqueeze(2).broadcast_to([128, 8, 64])
                self.tt(S_[:, :].rearrange("p (h m) -> p h m", m=64), S_[:, :].rearrange("p (h m) -> p h m", m=64),
                        elb, ALU.mult, reads=(rS, ("el", c)), writes=(rS,))
                self.tt(S_[:, :], S_[:, :], self.ps[bu][:, 0:512], ALU.add, reads=(rS, ("ps", bu)), writes=(rS,))
                if not last:
                    self.cp(Sb_[:, :], S_[:, :], reads=(rS,), writes=(rSb,), eng="act")
                if last:
                    if t["kind"] == "p" and not t["last"]:
                        self.dma(self.sstate[j, g], S_[:, :], reads=(rS,), writes=("sstate",))
                    else:
                        slot = 0 if t["kind"] == "p" else 1 + sg
                        dstst = self.o_ss[j, slot, g * 8:(g + 1) * 8].rearrange("h p n -> (h p) n") \
                            .rearrange("(b r) n -> r b n", r=128)
                        b = self.bank()
                        for bl in range(4):
                            self.tr(self.ps[b][:, bl * 128:(bl + 1) * 128], S_[:, bl * 128:(bl + 1) * 128],
                                    self.ident_f, reads=(rS, "cst"), writes=(("ps", b),))
                        self.cp(stg[:, :, :], self.ps[b][:, :].rearrange("p (b m) -> p b m", m=128),
                                reads=(("ps", b),), writes=("stg",), eng="act")
                        self.dma(dstst, stg[:, :, :], reads=("stg",), writes=())
            if _STOP == "chunk":
                raise StopBuild()
            for cc in range(4):
                for c in range(nck):
                    b = self.bank()
                    pb = self.ps[b][:, :].bitcast(BF16)
                    self.tr(pb[:, 0:Q], zs[0:Q, c, cc * 128:(cc + 1) * 128], self.ident_b[0:Q, 0:Q],
                            reads=(("zs", c), "identb"), writes=(("ps", b),))
                    self.act(gfm[:, cc, tok(c)], pb[:, 0:Q], AF.Copy, reads=(("ps", b), "prm"),
                             writes=(("gfm", cc),), scale=nw[:, g * 4 + cc:g * 4 + cc + 1])
            sl, sres = self.next_slab("ssd_out")
            for m in range(DC):
                b = self.bank()
                for kc in range(4):
                    self.mm(self.ps[b][:, 0:T], sl[:, kc, m * 128:(m + 1) * 128], gfm[:, kc, 0:T], start=(kc == 0),
                            stop=(kc == 3), reads=(sres, ("gfm", kc)), writes=(("ps", b),))
                self.tt(self.x[:, m, 0:T], self.x[:, m, 0:T], self.ps[b][:, 0:T], ALU.add,
                        reads=(("x", m), ("ps", b)), writes=(("x", m),))
            if _STOP == "g0":
                raise StopBuild()
        self.halo_store(t, "sc", l, 4, 48)


def build_program(cfg):
    p = Prog(cfg)
    poff, pn = param_layout(cfg)
    p.build_pre = None
    return p


def make_consts():
    c = np.zeros((128, 512), np.float32)
    c[:, 0:128] = np.eye(128, dtype=np.float32)
    c[:, 128:256] = np.triu(np.ones((128, 128), np.float32))
    c[:, 256:384] = np.where(np.arange(128)[:, None] <= np.arange(128)[None, :], 0.0, NEG)
    c[:, 384:512] = 1.0
    return c


def fm(v, nch):
    v = np.asarray(v, np.float32)
    lead = v.shape[:-1]
    return np.moveaxis(v.reshape(lead + (nch, 128)), -1, 0)


def pack_params(cfg, inp):
    poff, pn = param_layout(cfg)
    prm = np.zeros((128, pn), np.float32)

    def put(name, arr):
        o, n = poff[name]
        a = np.ascontiguousarray(arr, dtype=np.float32).reshape(128, -1)
        assert a.shape[1] == n, (name, a.shape, n)
        prm[:, o:o + n] = a
    NL, NCONV, NSSD = cfg.NL, cfg.NCONV, cfg.NSSD
    put("norm_mix", fm(inp["norm_mix"][:NL], DC))
    put("norm_ffn", fm(inp["norm_ffn"][:NL], DC))
    put("norm_final", fm(inp["norm_final"], DC))
    if NCONV:
        put("sc_conv_w", fm(inp["sc_conv_w"][:NCONV], DC))
    if NSSD:
        put("ssd_conv_w", fm(inp["ssd_conv_w"][:NSSD], 48))
        put("ssd_conv_b", fm(inp["ssd_conv_b"][:NSSD], 48))
        put("ssd_norm_w", fm(inp["ssd_norm_w"][:NSSD], 32))
        for nm, key in (("dt_bias", "ssd_dt_bias"), ("a_log", "ssd_a_log"), ("ssd_d", "ssd_d")):
            put(nm, np.broadcast_to(np.asarray(inp[key][:NSSD], np.float32)[None], (128, NSSD, 64)))
    put("ffn_conv_w", fm(inp["ffn_conv_w"][:NL], 88))
    put("ffn_conv_b", fm(inp["ffn_conv_b"][:NL], 88))
    return prm


_CACHE = {}


def run_cfg(cfg, inp, n_cores=8, trace=False):
    key = (cfg.NPT, cfg.TT, cfg.QP, cfg.NSEG, cfg.SL, cfg.NL)
    if key not in _CACHE:
        _CACHE[key] = Prog(cfg).build()
    nc = _CACHE[key]
    NL, NCONV, NSSD, NSEG = cfg.NL, cfg.NCONV, cfg.NSSD, cfg.NSEG
    xp = np.asarray(inp["x_prompt"], np.float32)
    xs = np.asarray(inp["x_sample"], np.float32)
    B = xp.shape[0]
    meta = np.asarray(inp["meta_tokens"], np.float32)
    prm = pack_params(cfg, inp)
    cst = make_consts()
    f32 = lambda a: np.ascontiguousarray(a, dtype=np.float32)
    wts = {
        "sc_w_in": f32(inp["sc_w_in"][:max(NCONV, 1)]),
        "sc_w_out": f32(inp["sc_w_out"][:max(NCONV, 1)]),
        "ssd_w_in": f32(inp["ssd_w_in"][:max(NSSD, 1)]),
        "ssd_w_out": f32(inp["ssd_w_out"][:max(NSSD, 1)]),
        "ffn_w_up": f32(inp["ffn_w_up"][:NL]),
        "ffn_w_down": f32(inp["ffn_w_down"][:NL]),
    }
    in_maps = []
    for c in range(n_cores):
        b = c % B
        sl = slice(c * NSEG, (c + 1) * NSEG)
        xin = np.concatenate([meta, xp[b], xs[sl].reshape(NSEG * cfg.SL, D)], axis=0)
        assert xin.shape[0] == cfg.NTOK
        m = dict(wts)
        m.update({
            "xin": f32(xin), "cst": cst, "prm": prm,
            "st_ca": f32(inp["state_conv_a"][:max(NCONV, 1), sl]),
            "st_sc": f32(inp["state_ssd_conv"][:max(NSSD, 1), sl]),
            "st_ss": f32(inp["state_ssd"][:max(NSSD, 1), sl]),
            "st_ff": f32(inp["state_ffn_conv"][:NL, sl]),
        })
        in_maps.append(m)
    res = run_bass_kernel_spmd(nc, in_maps, core_ids=list(range(n_cores)), trace=trace)
    R = res.results
    nm = meta.shape[0]
    y_prompt = np.stack([R[b]["y"][nm:cfg.NPT] for b in range(B)])
    y_sample = np.concatenate([R[c]["y"][cfg.NPT:].reshape(NSEG, cfg.SL, D) for c in range(n_cores)])

    def pstate(k):
        return np.stack([R[b][k][:, 0] for b in range(B)], axis=1)

    def sstate(k):
        return np.concatenate([R[c][k][:, 1:] for c in range(n_cores)], axis=1)
    outs = (y_prompt, y_sample, pstate("o_ca")[:NCONV], pstate("o_sc")[:NSSD], pstate("o_ss")[:NSSD], pstate("o_ff"),
            sstate("o_ca")[:NCONV], sstate("o_sc")[:NSSD], sstate("o_ss")[:NSSD], sstate("o_ff"))
    return tuple(np.ascontiguousarray(o, dtype=np.float32) for o in outs), res


def kernel(**inputs):
    cfg = Cfg()
    outs, _ = run_cfg(cfg, inputs)
    return outs
```
